# Optimizing a Trainium2 kernel written in Bass

```python
import jax, jax.numpy as jnp
from jax import lax
import numpy as np

D_MODEL = 1024
BATCH = 2
SEQ = 8192
DEPTH = 1

CHUNK = 64
D_MIX = D_MODEL
D_GMLP = D_MIX // 2
D_HGRN = D_MIX - D_GMLP
GMLP_HEADS = 4
GMLP_HEAD_DIM = D_GMLP // GMLP_HEADS
GMLP_BLOCK = 128
HGRN_HEADS = 4
HGRN_HEAD_DIM = D_HGRN // HGRN_HEADS
D_FF = -(-(8 * D_MODEL) // (3 * 256)) * 256
N_ADA = 6
D_IN = 2 * D_GMLP + 4 * D_HGRN
EPS = 1e-6

kernel_name = "hybrid_gmlp_hgrn2_adaln_block"


def rmsnorm(x, w):
    xf = x.astype(jnp.float32)
    y = xf * lax.rsqrt(jnp.mean(xf * xf, axis=-1, keepdims=True) + EPS)
    return (y * w.astype(jnp.float32)).astype(x.dtype)


def layernorm(x, w, b):
    xf = x.astype(jnp.float32)
    mu = jnp.mean(xf, axis=-1, keepdims=True)
    var = jnp.mean(jnp.square(xf - mu), axis=-1, keepdims=True)
    y = (xf - mu) * lax.rsqrt(var + EPS)
    return (y * w.astype(jnp.float32) + b.astype(jnp.float32)).astype(x.dtype)


def modulate(h, shift, scale):
    return h * (1 + scale[:, None, :]) + shift[:, None, :]


def gmlp_spatial_gating(u, v, w_s, b_s, ln_w, ln_b):
    bsz, seq, _ = u.shape
    nb = seq // GMLP_BLOCK
    u = jax.nn.gelu(u, approximate=False)
    v = layernorm(jax.nn.gelu(v, approximate=False), ln_w, ln_b)
    vb = v.reshape(bsz, nb, GMLP_BLOCK, GMLP_HEADS, GMLP_HEAD_DIM)
    cid = jnp.arange(GMLP_BLOCK) // CHUNK
    mask = cid[:, None] >= cid[None, :]
    ws = jnp.where(mask[None], w_s, 0).astype(v.dtype)
    mixed = jnp.einsum('hts,bnshc->bnthc', ws, vb) + b_s.T.astype(v.dtype)[None, None, :, :, None]
    return u * mixed.reshape(bsz, seq, D_GMLP)


def hgrn2_recurrence(q, f_logit, inp, g, lb, gn_w):
    dtype = q.dtype
    bsz, seq, _ = q.shape
    nc = seq // CHUNK
    qf = jax.nn.silu(q.astype(jnp.float32))
    f = lb + (1.0 - lb) * jax.nn.sigmoid(f_logit.astype(jnp.float32))
    logf = jnp.log(f)
    k = 1.0 - f
    vf = inp.astype(jnp.float32)

    def to_chunks(t):
        return t.reshape(bsz, nc, CHUNK, HGRN_HEADS, HGRN_HEAD_DIM).transpose(1, 0, 3, 2, 4)

    tri = jnp.arange(CHUNK)[:, None] >= jnp.arange(CHUNK)[None, :]

    def step(state, xs):
        qc, kc, vc, lc = xs
        b = jnp.cumsum(lc, axis=2)
        inter = jnp.einsum('bhtd,bhde->bhte', qc * jnp.exp(b), state)
        diff = b[:, :, :, None, :] - b[:, :, None, :, :]
        decay = jnp.where(tri[:, :, None], jnp.exp(jnp.minimum(diff, 0.0)), 0.0)
        attn = jnp.einsum('bhtd,bhsd,bhtsd->bhts', qc, kc, decay)
        intra = jnp.einsum('bhts,bhse->bhte', attn, vc)
        b_last = b[:, :, -1, :]
        state = state * jnp.exp(b_last)[..., None] + jnp.einsum(
            'bhsd,bhse->bhde', kc * jnp.exp(b_last[:, :, None, :] - b), vc)
        return state, inter + intra

    s0 = jnp.zeros((bsz, HGRN_HEADS, HGRN_HEAD_DIM, HGRN_HEAD_DIM), jnp.float32)
    _, o = lax.scan(step, s0, (to_chunks(qf), to_chunks(k), to_chunks(vf), to_chunks(logf)))
    o = o.transpose(1, 0, 3, 2, 4).reshape(bsz, seq, HGRN_HEADS, HGRN_HEAD_DIM)
    gate = jax.nn.silu(g.astype(jnp.float32)).reshape(bsz, seq, HGRN_HEADS, HGRN_HEAD_DIM)
    o = rmsnorm(o, gn_w) * gate
    return o.reshape(bsz, seq, D_HGRN).astype(dtype)


def setup_inputs(seed: int = 0) -> dict:
    key = jax.random.key(seed)
    ks = jax.random.split(key, 20)
    nrm = lambda k, shape, s: jax.random.normal(k, shape, jnp.float32) * s
    return {
        "x": nrm(ks[0], (BATCH, SEQ, D_MODEL), 1.0),
        "c": nrm(ks[1], (BATCH, D_MODEL), 1.0),
        "w_ada": nrm(ks[2], (DEPTH, D_MODEL, N_ADA * D_MODEL), 0.5 * D_MODEL ** -0.5),
        "b_ada": nrm(ks[3], (DEPTH, N_ADA * D_MODEL), 0.02),
        "norm1_w": 1.0 + nrm(ks[4], (DEPTH, D_MODEL), 0.02),
        "w_in": nrm(ks[5], (DEPTH, D_MODEL, D_IN), D_MODEL ** -0.5),
        "w_s": nrm(ks[6], (DEPTH, GMLP_HEADS, GMLP_BLOCK, GMLP_BLOCK), GMLP_BLOCK ** -0.5),
        "b_s": 1.0 + nrm(ks[7], (DEPTH, GMLP_HEADS, GMLP_BLOCK), 0.02),
        "v_ln_w": 1.0 + nrm(ks[8], (DEPTH, D_GMLP), 0.02),
        "v_ln_b": nrm(ks[9], (DEPTH, D_GMLP), 0.02),
        "lower_bounds": nrm(ks[10], (DEPTH + 1, D_HGRN), 0.5),
        "gn_w": 1.0 + nrm(ks[11], (DEPTH, HGRN_HEAD_DIM), 0.02),
        "w_out": nrm(ks[12], (DEPTH, D_MIX, D_MODEL), D_MIX ** -0.5),
        "norm2_w": 1.0 + nrm(ks[13], (DEPTH, D_MODEL), 0.02),
        "w_ffn_in": nrm(ks[14], (DEPTH, D_MODEL, 2 * D_FF), D_MODEL ** -0.5),
        "w_ffn_out": nrm(ks[15], (DEPTH, D_FF, D_MODEL), D_FF ** -0.5),
        "final_norm_w": 1.0 + nrm(ks[16], (D_MODEL,), 0.02),
    }


def reference(x, c, w_ada, b_ada, norm1_w, w_in, w_s, b_s, v_ln_w, v_ln_b,
              lower_bounds, gn_w, w_out, norm2_w, w_ffn_in, w_ffn_out, final_norm_w):
    lb_all = jnp.cumsum(jax.nn.softmax(lower_bounds.astype(jnp.float32), axis=0), axis=0)
    c_act = jax.nn.silu(c)
    split_at = [D_GMLP, 2 * D_GMLP, 2 * D_GMLP + D_HGRN,
                2 * D_GMLP + 2 * D_HGRN, 2 * D_GMLP + 3 * D_HGRN]
    for l in range(DEPTH):
        ada = (c_act @ w_ada[l] + b_ada[l]).astype(x.dtype)
        sh1, sc1, g1, sh2, sc2, g2 = jnp.split(ada, N_ADA, axis=-1)

        h = modulate(rmsnorm(x, norm1_w[l]), sh1, sc1)
        proj = h @ w_in[l]
        u, v, q, f_logit, inp, g = jnp.split(proj, split_at, axis=-1)
        y_a = gmlp_spatial_gating(u, v, w_s[l], b_s[l], v_ln_w[l], v_ln_b[l])
        y_b = hgrn2_recurrence(q, f_logit, inp, g, lb_all[l], gn_w[l])
        mix = jnp.concatenate([y_a, y_b], axis=-1) @ w_out[l]
        x = x + g1[:, None, :] * mix

        h = modulate(rmsnorm(x, norm2_w[l]), sh2, sc2)
        gate, up = jnp.split(h @ w_ffn_in[l], 2, axis=-1)
        x = x + g2[:, None, :] * ((jax.nn.silu(gate) * up) @ w_ffn_out[l])
    return rmsnorm(x, final_norm_w)
```

```python
from contextlib import ExitStack
import numpy as np
import ml_dtypes
import concourse.bass as bass
import concourse.mybir as mybir
from concourse.bass_utils import run_bass_kernel_spmd

F32 = mybir.dt.float32
BF16 = mybir.dt.bfloat16
I32 = mybir.dt.int32
AF = mybir.ActivationFunctionType
ALU = mybir.AluOpType

import os
STAGE = float(os.environ.get('KSTAGE', '9'))
SERIAL = int(os.environ.get('KSERIAL', '0'))
REORDER = int(os.environ.get('KREORDER', '1'))
NCORES = 8
D = 1024
SEQ = 8192
TOK = 2048
NT = 16
NG = 8
GT = 256
TPG = 2
DFF = 2816
NCH = 22
NEWTON_ITERS = 2
NPREV = 48
EPS = 1e-6
FFN_PASSES = [(0, 4), (4, 4), (8, 4), (12, 4), (16, 3), (19, 3)]


class _Op:
    __slots__ = ("eng", "fn", "reads", "writes", "dma", "idx", "pos", "deps",
                 "signal", "sigval", "dcount", "grp", "gcount", "inc")

    def __init__(self, eng, fn, reads, writes, dma):
        self.eng, self.fn, self.reads, self.writes, self.dma = eng, fn, reads, writes, dma
        self.deps = []
        self.signal = False
        self.sigval = None
        self.dcount = None


class Sched:
    ENGS = ("pe", "act", "dve", "pool", "sp")
    PSUM_BANKS = ("PB0", "PB1", "PB2", "PT0", "PT1", "PA", "PO", "PU")

    def __init__(self, nc):
        self.nc = nc
        self.ops = []

    def op(self, eng, fn, reads=(), writes=(), dma=None, grp=None, inc=16):
        o = _Op(eng, fn, tuple(reads), tuple(writes), dma)
        o.inc = inc
        o.grp = grp
        o.gcount = None
        o.idx = len(self.ops)
        self.ops.append(o)
        return o

    COST = {"pe": 0.17, "act": 0.47, "dve": 0.41, "pool": 0.7, "sp": 0.05}

    def reorder(self):
        import heapq
        ops = self.ops
        n = len(ops)
        last_w, readers = {}, {}
        preds = [set() for _ in range(n)]
        last_key = {}
        for o in ops:
            i = o.idx
            if o.dma is not None:
                if o.dma in last_key:
                    preds[i].add(last_key[o.dma])
                last_key[o.dma] = i
            elif o.eng == "pe":
                for b in o.writes:
                    if isinstance(b, tuple) and b[0] in self.PSUM_BANKS:
                        k = ("pebank", b[0])
                        if k in last_key and last_key[k] != i:
                            preds[i].add(last_key[k])
                        last_key[k] = i
            for b in o.reads:
                if b in last_w:
                    preds[i].add(last_w[b])
            for b in o.writes:
                if b in last_w:
                    preds[i].add(last_w[b])
                for r in readers.get(b, ()):
                    preds[i].add(r)
            for b in o.reads:
                readers.setdefault(b, []).append(i)
            for b in o.writes:
                last_w[b] = i
                readers[b] = []
            preds[i].discard(i)
        members = {}
        for o in ops:
            if o.dma is not None and o.grp is not None:
                members.setdefault((o.dma, o.grp), []).append(o.idx)
        for i in range(n):
            extra = set()
            for j in preds[i]:
                p = ops[j]
                if p.dma is not None and p.grp is not None and not (ops[i].dma == p.dma and ops[i].grp == p.grp):
                    extra.update(members[(p.dma, p.grp)])
            extra.discard(i)
            preds[i] |= extra
        succs = [[] for _ in range(n)]
        indeg = [0] * n
        for i in range(n):
            indeg[i] = len(preds[i])
            for j in preds[i]:
                succs[j].append(i)
        def cost_of(o):
            return 3.0 if o.dma is not None else self.COST[o.eng] + 0.15
        blevel = [0.0] * n
        for i in range(n - 1, -1, -1):
            m = 0.0
            for k in succs[i]:
                if blevel[k] > m:
                    m = blevel[k]
            blevel[i] = m + cost_of(ops[i])
        finish = [0.0] * n
        ready_t = [0.0] * n
        eng_free = {e: 0.0 for e in self.ENGS}
        ready = {e: [] for e in self.ENGS}
        for i in range(n):
            if indeg[i] == 0:
                ready[ops[i].eng].append(i)
        order = []
        done = 0
        while done < n:
            best = None
            for e in self.ENGS:
                if not ready[e]:
                    continue
                tmin = min(ready_t[i] for i in ready[e])
                st = max(tmin, eng_free[e])
                if best is None or st < best[0]:
                    best = (st, e)
            st, e = best
            cands = [i for i in ready[e] if ready_t[i] <= st + 1e-9]
            i = max(cands, key=lambda i: (blevel[i], -i))
            ready[e].remove(i)
            o = ops[i]
            if o.dma is not None:
                eng_free[e] = st + 0.05
                finish[i] = st + 3.0
            else:
                eng_free[e] = st + self.COST[e]
                finish[i] = eng_free[e] + 0.15
            order.append(i)
            done += 1
            for k in succs[i]:
                ready_t[k] = max(ready_t[k], finish[i])
                indeg[k] -= 1
                if indeg[k] == 0:
                    ready[ops[k].eng].append(k)
        self.ops = [ops[i] for i in order]
        for k, o in enumerate(self.ops):
            o.idx = k
        self.sim_span = max(finish)

    def emit(self, stack):
        if REORDER:
            self.reorder()
        nc = self.nc
        ops = self.ops
        last_w = {}
        readers = {}
        pos_ctr = {e: 0 for e in self.ENGS}
        waited = {e: {p: 0 for p in self.ENGS} for e in self.ENGS}
        waited_dma = {e: {} for e in self.ENGS}
        dma_ctr = {}
        gmax = {}
        for o in ops:
            if o.dma is not None:
                dma_ctr[o.dma] = dma_ctr.get(o.dma, 0) + 1
                o.dcount = dma_ctr[o.dma]
                if o.grp is not None:
                    gmax[(o.dma, o.grp)] = o.dcount
        for o in ops:
            if o.dma is not None:
                o.gcount = gmax[(o.dma, o.grp)] if o.grp is not None else o.dcount
        bank_last = {b: {} for b in self.PSUM_BANKS}
        for o in ops:
            if o.dma is None:
                pos_ctr[o.eng] += 1
                o.pos = pos_ctr[o.eng]
            raw = set()
            war = set()
            if SERIAL and o.idx > 0:
                raw.add(o.idx - 1)
            banks = set()
            for b in o.reads + o.writes:
                if isinstance(b, tuple) and b[0] in bank_last:
                    banks.add(b[0])
            for bk in banks:
                for e2, j2 in bank_last[bk].items():
                    if e2 != o.eng:
                        raw.add(j2)
                bank_last[bk][o.eng] = o.idx
            for b in o.reads:
                if b in last_w:
                    raw.add(last_w[b])
            for b in o.writes:
                if b in last_w:
                    raw.add(last_w[b])
                for r in readers.get(b, ()):
                    war.add(r)
            deps = []
            for j in sorted(raw | war):
                p = ops[j]
                if j == o.idx:
                    continue
                if p.dma is not None:
                    k = p.dma
                    if o.dma == p.dma and o.grp is not None and o.grp == p.grp:
                        continue
                    if waited_dma[o.eng].get(k, 0) >= p.gcount:
                        continue
                    waited_dma[o.eng][k] = p.gcount
                    deps.append(j)
                    continue
                if p.eng == o.eng and o.dma is None:
                    if o.eng == "pe":
                        continue
                if waited[o.eng][p.eng] >= p.pos:
                    continue
                waited[o.eng][p.eng] = p.pos
                p.signal = True
                deps.append(j)
            o.deps = deps
            for b in o.reads:
                readers.setdefault(b, []).append(o.idx)
            for b in o.writes:
                last_w[b] = o.idx
                readers[b] = []
        consumed = set()
        for o in ops:
            consumed.update(o.deps)
        tail_dmas = [o for o in ops if o.dma is not None and o.idx not in consumed]
        sig_ctr = {e: 0 for e in self.ENGS}
        for o in ops:
            if o.dma is None and o.signal:
                sig_ctr[o.eng] += 1
                o.sigval = sig_ctr[o.eng]
        sems = {e: stack.enter_context(nc.semaphore("s_" + e)) for e in self.ENGS}
        dsems = {k: stack.enter_context(nc.semaphore("d_" + str(k))) for k in dma_ctr}
        engobj = {"pe": nc.tensor, "act": nc.scalar, "dve": nc.vector,
                  "pool": nc.gpsimd, "sp": nc.sync}
        for o in ops:
            e = engobj[o.eng]
            for j in o.deps:
                p = ops[j]
                if p.dma is not None:
                    e.wait_ge(dsems[p.dma], p.inc * p.gcount)
                else:
                    e.wait_ge(sems[p.eng], p.sigval)
            ins = o.fn(e)
            if o.dma is not None:
                ins.then_inc(dsems[o.dma], o.inc)
            elif o.signal:
                ins.then_inc(sems[o.eng], 1)
        done = set()
        for o in tail_dmas:
            if o.dma in done:
                continue
            done.add(o.dma)
            nc.sync.wait_ge(dsems[o.dma], 16 * dma_ctr[o.dma])


def Q(name):
    return [(name, q) for q in range(4)]


def build_nc():
    nc = bass.Bass("TRN2", target_bir_lowering=False)

    def din(name, shape, dt=F32):
        return nc.dram_tensor(name, list(shape), dt, kind="ExternalInput")

    x_d = din("x", [TOK, D])
    ccol_d = din("c_col", [128, 8])
    wada_d = din("w_ada", [D, 6 * D])
    badac_d = din("b_ada_col", [128, 48])
    badab_d = din("b_ada_bc", [128, 2 * D])
    n1c_d = din("n1_col", [128, 8])
    n2c_d = din("n2_col", [128, 8])
    win_d = din("w_in", [D, 3072])
    wst_d = din("wsT", [128, 512])
    bsbc_d = din("bs_bc", [128, 512])
    lnw_d = din("lnw_col", [128, 4])
    lnb_d = din("lnb_col", [128, 4])
    lba0_d = din("lba0", [128, 4])
    lba1_d = din("lba1", [128, 4])
    gnw_d = din("gnw_bc", [128, 512])
    wout_d = din("w_out", [D, D])
    wfi_d = din("w_ffn_in", [D, 2 * DFF])
    wfo_d = din("w_ffn_out", [DFF, D])
    fnw_d = din("fnw_bc", [128, D])
    ident_d = din("ident", [128, 128], BF16)
    cmask_d = din("cmask", [128, 128], I32)
    gmask_d = din("gmask", [128, 128])
    pmask_d = din("pmask", [128, 8])
    xprev_d = din("xprev", [NPREV * 128, D])
    pmt_d = din("pmt", [128, NPREV])
    out_d = nc.dram_tensor("out", [TOK, D], F32, kind="ExternalOutput")
    DUMPS = [d for d in os.environ.get('KDUMP', '').split(',') if d]
    dbg_d = nc.dram_tensor("dbg", [max(1, len(DUMPS)) * 128, 512], F32, kind="ExternalOutput") if DUMPS else None
    ccin_d = nc.dram_tensor("cc_in", [128, 516], F32)
    ccout_d = nc.dram_tensor("cc_out", [NCORES * 128, 516], F32)

    S = Sched(nc)
    en = [True]
    POOLC = os.environ.get('KPOOL', 'pool')

    def op(*a, **k):
        if en[0]:
            return S.op(*a, **k)
    with ExitStack() as st:
        def sb(name, shape, dt=F32):
            return st.enter_context(nc.sbuf_tensor(name, list(shape), dt))

        def ps(name, shape, dt=F32):
            return st.enter_context(nc.psum_tensor(name, list(shape), dt))

        X = sb("X", [128, NT, D])
        HT = sb("HT", [128, 8, TOK], BF16)
        WINF = sb("WINF", [128, 8 * 3072], BF16)
        WIN = WINF[:].rearrange("p (k c) -> p k c", k=8)
        WOUT = sb("WOUT", [128, 8, D], BF16)
        IDB = sb("IDB", [128, 128], BF16)
        ONESB = sb("ONESB", [128, 128], BF16)
        CMASK = sb("CMASK", [128, 128], I32)
        GMASK = sb("GMASK", [128, 128])
        PMT = sb("PMT", [128, NPREV])
        WST = sb("WST", [128, 4, 128], BF16)
        BIAS2 = sb("BIAS2", [128, 4, 128])
        GNWBC = sb("GNWBC", [128, 512])
        GBC = sb("GBC", [128, D])
        COLS = sb("COLS", [128, 128])
        col_ctr = [0]

        def cols(n):
            a = col_ctr[0]
            col_ctr[0] += n
            assert col_ctr[0] <= 128
            return COLS[:, a:a + n]
        CCOL = cols(8); CTH = cols(8); N1C = cols(8); N2C = cols(8)
        A1C = cols(8); SH1C = cols(8); A2C = cols(8); SH2C = cols(8)
        SC1C = cols(8); SC2C = cols(8)
        LNWC = cols(4); LNBC = cols(4); LBA0 = cols(4); LBA1 = cols(4)
        LBC = cols(4); C0C = cols(4); C1C = cols(4); NC1C = cols(4)
        DT = cols(4); DMC = cols(4)
        BADAC = sb("BADAC", [128, 48])
        CACTB = sb("CACTB", [128, 8], BF16)
        CREP = sb("CREP", [128, 8, 128], BF16)
        C63 = sb("C63", [128, NT * 4]); C127 = sb("C127", [128, NT * 4])
        R1 = sb("R1", [128, NT * 4]); PLAST = sb("PLAST", [128, NT * 4])
        SS1 = sb("SS1", [128, NT]); RS1 = sb("RS1", [128, NT])
        NWV = sb("NWV", [128, 16]); NWT = sb("NWT", [128, 16])
        SSO = sb("SSO", [128, 4]); RSO = sb("RSO", [128, 4])
        SS2 = sb("SS2", [128, 1]); RS2 = sb("RS2", [128, 1])
        BNS = sb("BNS", [128, 6]); MV = sb("MV", [128, 2]); RSV = sb("RSV", [128, 1])
        SST = sb("SST", [128, 512])
        SBF = sb("SBF", [128, 512], BF16)
        SSP = sb("SSP", [128, NPREV]); RSP = sb("RSP", [128, NPREV])
        NWV2 = SSP; NWT2 = sb("NWT2", [128, NPREV])
        XN = sb("XN", [128, D], BF16)
        SQJ = sb("SQJ", [128, D], BF16)
        GU = sb("GU", [128, 4, GT], BF16)
        KE = sb("KE", [128, 4, GT], BF16)
        QB = sb("QB", [128, 4, GT], BF16)
        YT = sb("YT", [128, 8, GT], BF16)
        TH = sb("TH", [128, GT]); KK = sb("KK", [128, GT]); PM = sb("PM", [128, GT])
        TH2 = sb("TH2", [128, GT]); KK2 = sb("KK2", [128, GT])
        THQ = sb("THQ", [128, GT])
        GV = sb("GV", [128, 512])
        TMP = sb("TMP", [128, 512])
        VH = sb("VH", [128, 512], BF16); VT = sb("VT", [128, 512], BF16)
        YB = sb("YB", [128, 512], BF16)
        KET = sb("KET", [128, 512], BF16); ATM = sb("ATM", [128, 4, 128], BF16)
        ZER = sb("ZER", [128, 64])
        PB = [ps("PB%d" % i, [128, 512]) for i in range(3)]
        PT = [ps("PT%d" % i, [128, 8, 128], BF16) for i in range(2)]
        PA = ps("PA", [128, 512]); PO = ps("PO", [128, 512]); PU = ps("PU", [128, 512])
        bigs = [(PB[0], "PB0"), (PB[1], "PB1"), (PB[2], "PB2")]
        big_ctr = [0]

        def big():
            b = bigs[big_ctr[0] % len(bigs)]
            big_ctr[0] += 1
            return b
        pt_ctr = [0]

        def ptb():
            i = pt_ctr[0] % 2
            pt_ctr[0] += 1
            return PT[i], "PT%d" % i
        dk = [0]

        def dkey(prefix):
            dk[0] += 1
            return "%s%d" % (prefix, dk[0])

        def htr(t):
            return [("HT", t, kc) for kc in range(8)]

        def htg(g, kc):
            return [("HT", g * TPG + j, kc) for j in range(TPG)]

        def newton_rsqrt(v, y, t, vid, yid, tid):
            vi = v.bitcast(I32); yi = y.bitcast(I32)
            op("dve", lambda e: e.tensor_scalar(yi, vi, 1, None, ALU.arith_shift_right),
               reads=[vid], writes=[yid])
            op("dve", lambda e: e.tensor_scalar(yi, yi, -1, 0x5f3759df, ALU.mult, ALU.add),
               reads=[yid], writes=[yid])
            for _ in range(NEWTON_ITERS):
                op("dve", lambda e: e.tensor_tensor(t, y, y, ALU.mult), reads=[yid], writes=[tid])
                op("dve", lambda e: e.tensor_tensor(t, t, v, ALU.mult), reads=[tid, vid], writes=[tid])
                op("dve", lambda e: e.tensor_scalar(t, t, -0.5, 1.5, ALU.mult, ALU.add),
                   reads=[tid], writes=[tid])
                op("dve", lambda e: e.tensor_tensor(y, y, t, ALU.mult), reads=[tid, yid], writes=[yid])

        def load(eng, dst, src, wid, prefix="l"):
            op(eng, lambda e: e.dma_start(out=dst, in_=src), writes=wid, dma=dkey(prefix))

        en[0] = STAGE >= 0.3
        load("sp", CCOL, ccol_d[:], ["CCOL"])
        load("sp", N1C, n1c_d[:], ["N1C"])
        load("sp", N2C, n2c_d[:], ["N2C"])
        load("sp", LNWC, lnw_d[:], ["LNWC"])
        load("sp", LNBC, lnb_d[:], ["LNBC"])
        load("sp", LBA0, lba0_d[:], ["LBA0"])
        load("sp", LBA1, lba1_d[:], ["LBA1"])
        load("sp", BADAC[:], badac_d[:], ["BADAC"])
        load("sp", IDB[:], ident_d[:], ["IDB"])
        load("sp", CMASK[:], cmask_d[:], ["CMASK"])
        load("sp", GMASK[:], gmask_d[:], ["GMASK"])
        load("sp", PMT[:], pmt_d[:], ["PMT"])
        load("sp", GNWBC[:], gnw_d[:], ["GNWBC"])
        en[0] = STAGE >= 0.5
        WADA_HT = [HT[:, :, 0:1024], HT[:, :, 1024:2048]]
        HT_IDS = [[("HT", t, kc) for t in range(0, 8) for kc in range(8)],
                  [("HT", t, kc) for t in range(8, 16) for kc in range(8)]]

        gctr = [0]

        def wada_to_ht(blk, half):
            gctr[0] += 1
            for kc in range(8):
                for hh in range(2):
                    op("pool", lambda e, kc=kc, hh=hh: e.dma_start(
                        out=HT[:, kc, half * 1024 + hh * 512:half * 1024 + (hh + 1) * 512],
                        in_=wada_d[kc * 128:(kc + 1) * 128, blk * D + hh * 512:blk * D + (hh + 1) * 512]),
                       writes=[("HT", t, kc) for t in range(half * 8 + hh * 4, half * 8 + hh * 4 + 4)],
                       dma="wada_ht%d" % half, grp=gctr[0])

        def to_wout(src_ap_fn, key):
            gctr[0] += 1
            for kc in range(8):
                for hh in range(2):
                    op("pool", lambda e, kc=kc, hh=hh: e.dma_start(out=WOUT[:, kc, hh * 512:(hh + 1) * 512],
                                                                   in_=src_ap_fn(kc)[:, hh * 512:(hh + 1) * 512]),
                       reads=[("RSP", 0)], writes=[("WOUT", kc, hh)], dma=key, grp=gctr[0])

        def wada_rows(blk):
            return lambda kc: wada_d[kc * 128:(kc + 1) * 128, blk * D:(blk + 1) * D]
        wada_to_ht(0, 0)
        wada_to_ht(1, 1)
        en[0] = STAGE >= 2
        for bq in range(NPREV // 16):
            bs = slice(bq * 16, (bq + 1) * 16)
            for p in range(bq * 16, (bq + 1) * 16):
                op("sp", lambda e, p=p: e.dma_start(out=X[:, p % NT, :], in_=xprev_d[p * 128:(p + 1) * 128, :]),
                   writes=[("X", p % NT)], dma="xprev0_%d" % (p % NT))
                op("act", lambda e, p=p: e.activation(XN[:], X[:, p % NT, :], AF.Square, accum_out=SSP[:, p:p + 1]),
                   reads=[("X", p % NT)], writes=["XN", ("SSP", p)])
            op("dve", lambda e, bs=bs: e.tensor_scalar(NWV2[:, bs], SSP[:, bs], 1.0 / D, EPS, ALU.mult, ALU.add),
               reads=[("SSP", p) for p in range(bq * 16, (bq + 1) * 16)], writes=[("NWV2", bq)])
            newton_rsqrt(NWV2[:, bs], RSP[:, bs], NWT2[:, bs], ("NWV2", bq), ("RSP", bq), ("NWT2", bq))
        en[0] = True
        for t in range(NT):
            load("act", X[:, t, :], x_d[t * 128:(t + 1) * 128, :], [("X", t)], "x")
        en[0] = STAGE >= 0.3

        def win_load(blk):
            gctr[0] += 1
            late = [("RSP", 0)] if blk not in (3, 4) else []
            for kc in range(8):
                op("pool", lambda e, kc=kc: e.dma_start(out=WIN[:, kc, blk * 512:(blk + 1) * 512],
                                                        in_=win_d[kc * 128:(kc + 1) * 128, blk * 512:(blk + 1) * 512]),
                   reads=late, writes=[("WIN", blk, kc)], dma="win%d" % blk, grp=gctr[0])

        op("pool", lambda e: e.memset(ONESB[:], 1.0), writes=["ONESB"])
        op("pool", lambda e: e.memset(ZER[:], 0.0), writes=["ZER"])
        op("pool", lambda e: e.memset(ATM[:], 0.0), writes=[("ATM", h) for h in range(4)])
        op("pool", lambda e: e.memset(SST[:], 0.0), writes=[("S", h) for h in range(4)])
        op("pool", lambda e: e.memset(DT, 1.0), writes=["DT"])

        op("act", lambda e: e.activation(CTH, CCOL, AF.Tanh, scale=0.5), reads=["CCOL"], writes=["CTH"])
        op("dve", lambda e: e.tensor_scalar(CTH, CTH, 0.5, 0.5, ALU.mult, ALU.add), reads=["CTH"], writes=["CTH"])
        op("dve", lambda e: e.tensor_tensor(CACTB[:], CCOL, CTH, ALU.mult), reads=["CCOL", "CTH"], writes=["CACTB"])
        for kc in range(8):
            op("dve", lambda e, kc=kc: e.tensor_scalar(CREP[:, kc, :], ONESB[:], CTH[:, kc:kc + 1], CCOL[:, kc:kc + 1],
                                                        ALU.mult, ALU.mult),
               reads=["ONESB", "CTH", "CCOL"], writes=[("CREP", kc)])
        op("dve", lambda e: e.tensor_tensor(LBC, LBA0, LBA1, ALU.subtract), reads=["LBA0", "LBA1"], writes=["LBC"])
        op("act", lambda e: e.activation(LBC, LBC, AF.Tanh, scale=0.5), reads=["LBC"], writes=["LBC"])
        op("dve", lambda e: e.tensor_scalar(LBC, LBC, 0.5, 0.5, ALU.mult, ALU.add), reads=["LBC"], writes=["LBC"])
        op("dve", lambda e: e.tensor_scalar(C1C, LBC, -0.5, 0.5, ALU.mult, ALU.add), reads=["LBC"], writes=["C1C"])
        op("dve", lambda e: e.tensor_scalar(C0C, LBC, 0.5, 0.5, ALU.mult, ALU.add), reads=["LBC"], writes=["C0C"])
        op("dve", lambda e: e.tensor_scalar(NC1C, C1C, -1.0, None, ALU.mult), reads=["C1C"], writes=["NC1C"])

        en[0] = STAGE >= 0.5
        def ada_cols(stage, sid, gi, dst, dstid):
            if os.environ.get('KNOADA'):
                return
            for half in range(2):
                pb, pid = big()
                for q in range(4):
                    fc = half * 4 + q
                    for kc in range(8):
                        op("pe", lambda e, fc=fc, kc=kc, q=q, pb=pb: e.matmul(pb[:, q * 128:(q + 1) * 128],
                                                                             stage[:, kc, fc * 128:(fc + 1) * 128],
                                                                             CREP[:, kc, :], start=(kc == 0), stop=(kc == 7)),
                           reads=list(sid(kc)) + [("CREP", kc)], writes=[(pid, q)])
                op("dve", lambda e, pb=pb: e.tensor_copy(TMP[:], pb[:]), reads=Q(pid), writes=Q("TMP"))
                for q in range(4):
                    fc = half * 4 + q
                    op("dve", lambda e, fc=fc, q=q: e.tensor_tensor(dst[:, fc:fc + 1], TMP[:, q * 128:q * 128 + 1],
                                                                    BADAC[:, gi * 8 + fc:gi * 8 + fc + 1], ALU.add),
                       reads=[("TMP", q), "BADAC"], writes=[dstid])

        def ada_bc(stage, sid, which):
            for half in range(2):
                load("sp", TMP[:], badab_d[:, which * D + half * 512: which * D + (half + 1) * 512], Q("TMP"))
                pb, pid = big()
                for kc in range(8):
                    op("pe", lambda e, kc=kc, half=half, pb=pb: e.matmul(pb[:], CREP[:, kc, :],
                                                                         stage[:, kc, half * 512:(half + 1) * 512],
                                                                         start=(kc == 0), stop=(kc == 7)),
                       reads=list(sid(kc)) + [("CREP", kc)], writes=Q(pid))
                op("dve", lambda e, half=half, pb=pb: e.tensor_tensor(GBC[:, half * 512:(half + 1) * 512], pb[:], TMP[:], ALU.add),
                   reads=Q(pid) + Q("TMP"), writes=[("GBC", half)])

        ada_cols(WADA_HT[0], lambda kc: [("HT", t, kc) for t in range(0, 8)], 0, SH1C, "SH1C")
        ada_cols(WADA_HT[1], lambda kc: [("HT", t, kc) for t in range(8, 16)], 1, SC1C, "SC1C")
        op("dve", lambda e: e.scalar_tensor_tensor(A1C, SC1C, 1.0, N1C, ALU.add, ALU.mult),
           reads=["SC1C", "N1C"], writes=["A1C"])
        win_load(3)
        win_load(4)

        en[0] = STAGE >= 0.7
        for t in range(NT):
            op("act", lambda e, t=t: e.activation(SQJ[:], X[:, t, :], AF.Square, accum_out=SS1[:, t:t + 1]),
               reads=[("X", t)], writes=["SQJ", ("SS1", t)])
        op("dve", lambda e: e.tensor_scalar(NWV[:], SS1[:], 1.0 / D, EPS, ALU.mult, ALU.add),
           reads=[("SS1", t) for t in range(NT)], writes=["NWV"])
        newton_rsqrt(NWV[:], RS1[:], NWT[:], "NWV", "RS1", "NWT")

        KHT = int(os.environ.get("KHT", "9"))

        def build_ht(t, rs_col, rsid, AC, SHC, acid, shid):
            op("act", lambda e: e.activation(XN[:], X[:, t, :], AF.Identity, scale=rs_col),
               reads=[("X", t), rsid], writes=["XN"])
            if KHT < 2:
                return
            pt, ptid = ptb()
            for kc in range(8):
                op("pe", lambda e, kc=kc: e.transpose(pt[:, kc, :], XN[:, kc * 128:(kc + 1) * 128], IDB[:]),
                   reads=["XN", "IDB"], writes=[(ptid, kc)])
            if KHT < 3:
                return
            for kc in range(8):
                dst = HT[:, kc, t * 128:(t + 1) * 128]
                if True:
                    op("act", lambda e, kc=kc, dst=dst: e.activation(dst, pt[:, kc, :], AF.Identity,
                                                                     bias=SHC[:, kc:kc + 1], scale=AC[:, kc:kc + 1]),
                       reads=[(ptid, kc), acid, shid], writes=[("HT", t, kc)])
                else:
                    op("dve", lambda e, kc=kc, dst=dst: e.tensor_scalar(dst, pt[:, kc, :], AC[:, kc:kc + 1],
                                                                        SHC[:, kc:kc + 1], ALU.mult, ALU.add),
                       reads=[(ptid, kc), acid, shid], writes=[("HT", t, kc)])

        en[0] = STAGE >= 1
        for t in range(NT):
            build_ht(t, RS1[:, t:t + 1], "RS1", A1C, SH1C, "A1C", "SH1C")

        en[0] = STAGE >= 1.5
        to_wout(wada_rows(3), "wo")
        ada_cols(WOUT, lambda kc: [("WOUT", kc, 0), ("WOUT", kc, 1)], 3, SH2C, "SH2C")
        to_wout(wada_rows(4), "wo")
        ada_cols(WOUT, lambda kc: [("WOUT", kc, 0), ("WOUT", kc, 1)], 4, SC2C, "SC2C")
        op("dve", lambda e: e.scalar_tensor_tensor(A2C, SC2C, 1.0, N2C, ALU.add, ALU.mult),
           reads=["SC2C", "N2C"], writes=["A2C"])
        for blk in (0, 1, 2, 5):
            win_load(blk)

        load("sp", GV[:], wst_d[:], Q("GV"))
        load("sp", TMP[:], bsbc_d[:], Q("TMP"))
        for h in range(4):
            hs = slice(h * 128, (h + 1) * 128)
            op("dve", lambda e, h=h, hs=hs: e.tensor_tensor(WST[:, h, :], GV[:, hs], GMASK[:], ALU.mult),
               reads=[("GV", h), "GMASK"], writes=[("WST", h)])
        pb, pid = big()
        for h in range(4):
            hs = slice(h * 128, (h + 1) * 128)
            op("pe", lambda e, h=h, hs=hs, pb=pb: e.matmul(pb[:, hs], ONESB[:], WST[:, h, :], start=True, stop=True),
               reads=["ONESB", ("WST", h)], writes=[(pid, h)])
            op("dve", lambda e, h=h, hs=hs, pb=pb: e.scalar_tensor_tensor(BIAS2[:, h, :], pb[:, hs], LNBC[:, h:h + 1],
                                                                          TMP[:, hs], ALU.mult, ALU.add),
               reads=[(pid, h), "LNBC", ("TMP", h)], writes=[("BIAS2", h)])

        def gcols(g):
            return slice(g * GT, (g + 1) * GT)

        def prev_buf(par, kc):
            if par == 0:
                return YT[:, kc, :], lambda j: ("YT", kc, j)
            if kc < 4:
                return GU[:, kc, :], lambda j: ("GU", kc)
            return QB[:, kc - 4, :], lambda j: ("QB", kc - 4)

        def hgrn_fm(g, main, prev=False, pbase=0, heads=(0, 1, 2, 3)):
            def src_ap(kc):
                return prev_buf(g % 2, kc)[0] if prev else HT[:, kc, gcols(g)]

            def src_ids(kc):
                return [prev_buf(g % 2, kc)[1](j) for j in range(TPG)] if prev else htg(g, kc)
            for h in heads:
                THx, thn = (TH, "TH") if h % 2 == 0 else (TH2, "TH2")
                KKx, kkn = (KK, "KK") if h % 2 == 0 else (KK2, "KK2")
                pz, pzid = big()
                for kc in range(8):
                    op("pe", lambda e, kc=kc, h=h, pz=pz, THx=THx, KKx=KKx: e.matmul(pz[:, 0:GT], WIN[:, kc, 1536 + h * 128:1536 + (h + 1) * 128],
                                                                   src_ap(kc), start=(kc == 0), stop=(kc == 7)),
                       reads=[("WIN", 3, kc)] + src_ids(kc), writes=Q(pzid))
                op("act", lambda e, pz=pz, THx=THx, KKx=KKx: e.activation(THx[:], pz[:, 0:GT], AF.Tanh, scale=0.5), reads=Q(pzid), writes=[thn])
                op("act", lambda e, h=h, THx=THx, KKx=KKx: e.activation(KKx[:], THx[:], AF.Identity, bias=C1C[:, h:h + 1], scale=NC1C[:, h:h + 1]),
                   reads=[thn, "NC1C", "C1C"], writes=[kkn])
                op("act", lambda e, h=h, THx=THx, KKx=KKx: e.activation(THx[:], THx[:], AF.Identity, bias=C0C[:, h:h + 1], scale=C1C[:, h:h + 1]),
                   reads=[thn, "C1C", "C0C"], writes=[thn])
                for b in range(2 * TPG):
                    sl = slice(b * 64, (b + 1) * 64)
                    op("dve", lambda e, sl=sl, THx=THx, KKx=KKx: e.tensor_tensor_scan(PM[:, sl], THx[:, sl], ZER[:], 1.0, ALU.mult, ALU.add),
                       reads=[thn, "ZER"], writes=[("PM", b)])
                for j in range(TPG):
                    idx = ((g * TPG + j) % NT) * 4 + h
                    c = slice(idx, idx + 1)
                    cid = ("CDEC", idx)
                    pmj = [("PM", 2 * j), ("PM", 2 * j + 1)]
                    op("dve", lambda e, j=j, c=c, THx=THx, KKx=KKx: e.tensor_copy(C63[:, c], PM[:, j * 128 + 63:j * 128 + 64]),
                       reads=pmj, writes=[cid])
                    op("dve", lambda e, j=j, c=c, THx=THx, KKx=KKx: e.tensor_copy(C127[:, c], PM[:, j * 128 + 127:j * 128 + 128]),
                       reads=pmj, writes=[cid])
                    op("dve", lambda e, c=c, THx=THx, KKx=KKx: e.reciprocal(R1[:, c], C63[:, c]), reads=[cid], writes=[cid])
                    op("dve", lambda e, c=c, THx=THx, KKx=KKx: e.tensor_tensor(PLAST[:, c], C63[:, c], C127[:, c], ALU.mult),
                       reads=[cid], writes=[cid])
                    op("dve", lambda e, j=j, c=c, THx=THx, KKx=KKx: e.tensor_scalar(PM[:, j * 128:j * 128 + 64], PM[:, j * 128:j * 128 + 64],
                                                                  R1[:, c], None, ALU.mult),
                       reads=[cid, ("PM", 2 * j)], writes=[("PM", 2 * j)])
                pmall = [("PM", b) for b in range(2 * TPG)]
                op("dve", lambda e, THx=THx, KKx=KKx: e.reciprocal(THx[:], PM[:]), reads=pmall, writes=[thn])
                op("dve", lambda e, h=h, THx=THx, KKx=KKx: e.tensor_tensor(KE[:, h, :], KKx[:], THx[:], ALU.mult),
                   reads=[kkn, thn], writes=[("KE", h)])
                if main:
                    pq, pqid = big()
                    for kc in range(8):
                        op("pe", lambda e, kc=kc, h=h, pq=pq: e.matmul(pq[:, 0:GT], WIN[:, kc, 1024 + h * 128:1024 + (h + 1) * 128],
                                                                       HT[:, kc, gcols(g)], start=(kc == 0), stop=(kc == 7)),
                           reads=[("WIN", 2, kc)] + htg(g, kc), writes=Q(pqid))
                    KQ = int(os.environ.get("KQ", "9"))
                    if KQ >= 2:
                        op("act", lambda e, pq=pq: e.activation(THQ[:], pq[:, 0:GT], AF.Tanh, scale=0.5), reads=Q(pqid), writes=["THQ"])
                    if KQ >= 3:
                        op("dve", lambda e: e.tensor_scalar(THQ[:], THQ[:], 0.5, 0.5, ALU.mult, ALU.add),
                           reads=["THQ"], writes=["THQ"])
                    if KQ >= 4:
                        op("dve", lambda e, pq=pq: e.tensor_tensor(THQ[:], pq[:, 0:GT], THQ[:], ALU.mult),
                           reads=Q(pqid) + ["THQ"], writes=["THQ"])
                    if KQ >= 5:
                        op("dve", lambda e, h=h: e.tensor_tensor(QB[:, h, :], THQ[:], PM[:], ALU.mult),
                           reads=["THQ"] + pmall, writes=[("QB", h)])

        def hgrn_tile(g, j, main, prev=False):
            t = g * TPG + j
            lc = slice(j * 128, (j + 1) * 128)
            def tsrc_ap(kc):
                return prev_buf(g % 2, kc)[0][:, lc] if prev else HT[:, kc, t * 128:(t + 1) * 128]

            def tid(kc):
                return prev_buf(g % 2, kc)[1](j) if prev else ("HT", t, kc)
            pv, pvid = big()
            for kc in range(8):
                op("pe", lambda e, kc=kc, pv=pv: e.matmul(pv[:], tsrc_ap(kc), WIN[:, kc, 2048:2560],
                                                          start=(kc == 0), stop=(kc == 7)),
                   reads=[("WIN", 4, kc), tid(kc)], writes=Q(pvid))
            if prev:
                pc = slice(g * TPG + j, g * TPG + j + 1)
                op("act", lambda e, pv=pv, pc=pc: e.activation(VT[:], pv[:], AF.Identity, scale=PMT[:, pc]),
                   reads=Q(pvid) + ["PMT"], writes=["VT"])
            else:
                op("act", lambda e, pv=pv: e.copy(VT[:], pv[:]), reads=Q(pvid), writes=["VT"])
            pt, ptid = ptb()
            for h in range(4):
                op("pe", lambda e, h=h, pt=pt: e.transpose(pt[:, h, :], KE[:, h, lc], IDB[:]),
                   reads=[("KE", h), "IDB"], writes=[(ptid, h)])
            op("dve", lambda e, pt=pt: e.tensor_copy(KET[:].rearrange("p (h c) -> p h c", h=4), pt[:, 0:4, :]),
               reads=[(ptid, h) for h in range(4)], writes=["KET"])
            for h in range(4):
                hs = slice(h * 128, (h + 1) * 128)
                idx = (t % NT) * 4 + h
                c = slice(idx, idx + 1)
                cid = ("CDEC", idx)
                op("pe", lambda e, hs=hs: e.matmul(PU[:, hs], KET[:, hs], VT[:, hs], start=True, stop=True),
                   reads=["KET", "VT"], writes=[("PU", h)])
                if main:
                    op("pe", lambda e, h=h, hs=hs: e.matmul(PA[:, hs], KE[:, h, lc], QB[:, h, lc], start=True, stop=True),
                       reads=[("KE", h), ("QB", h)], writes=[("PA", h)])
                    op("dve", lambda e, h=h, hs=hs: e.copy_predicated(ATM[:, h, :], CMASK[:], PA[:, hs]),
                       reads=[("PA", h), "CMASK"], writes=[("ATM", h)])
                    op("dve", lambda e, hs=hs, c=c: e.tensor_scalar(SBF[:, hs], SST[:, hs], C63[:, c], None, ALU.mult),
                       reads=[("S", h), cid], writes=[("SBF", h)])
                    op("pe", lambda e, h=h, hs=hs: e.matmul(PO[:, hs], ATM[:, h, :], VT[:, hs], start=True, stop=False),
                       reads=[("ATM", h), "VT"], writes=[("PO", h)])
                    op("pe", lambda e, h=h, hs=hs: e.matmul(PO[:, hs], QB[:, h, lc], SBF[:, hs], start=False, stop=True),
                       reads=[("QB", h), ("SBF", h)], writes=[("PO", h)])
                op("dve", lambda e, hs=hs, c=c: e.tensor_scalar(TMP[:, hs], PU[:, hs], C127[:, c], None, ALU.mult),
                   reads=[("PU", h), cid], writes=[("TMP", h)])
                op("dve", lambda e, hs=hs, c=c: e.scalar_tensor_tensor(SST[:, hs], SST[:, hs], PLAST[:, c], TMP[:, hs],
                                                                       ALU.mult, ALU.add),
                   reads=[("S", h), cid, ("TMP", h)], writes=[("S", h)])

        def rev(ap, n):
            return bass.AP(ap.tensor, ap.offset + n - 1, [list(ap.ap[0]), [-1, n]])

        def zeros_ap(n):
            z = ZER[:, 0:1]
            return bass.AP(z.tensor, z.offset, [list(z.ap[0]), [0, n]])

        def hgrn_fm_prev(pg_, heads):
            for h in heads:
                THx, thn = (TH, "TH") if h % 2 == 0 else (TH2, "TH2")
                KKx, kkn = (KK, "KK") if h % 2 == 0 else (KK2, "KK2")
                pz, pzid = big()
                for kc in range(8):
                    sap, sidf = prev_buf(pg_ % 2, kc)
                    op("pe", lambda e, kc=kc, h=h, pz=pz, sap=sap: e.matmul(pz[:, 0:GT], WIN[:, kc, 1536 + h * 128:1536 + (h + 1) * 128],
                                                                            sap, start=(kc == 0), stop=(kc == 7)),
                       reads=[("WIN", 3, kc)] + [sidf(j) for j in range(TPG)], writes=Q(pzid))
                op("act", lambda e, pz=pz, THx=THx: e.activation(THx[:], pz[:, 0:GT], AF.Tanh, scale=0.5), reads=Q(pzid), writes=[thn])
                op("act", lambda e, h=h, THx=THx, KKx=KKx: e.activation(KKx[:], THx[:], AF.Identity, bias=C1C[:, h:h + 1], scale=NC1C[:, h:h + 1]),
                   reads=[thn, "NC1C", "C1C"], writes=[kkn])
                op("act", lambda e, h=h, THx=THx: e.activation(THx[:], THx[:], AF.Identity, bias=C0C[:, h:h + 1], scale=C1C[:, h:h + 1]),
                   reads=[thn, "C1C", "C0C"], writes=[thn])
                for j in range(TPG):
                    t0 = j * 128
                    idx = ((pg_ * TPG + j) % NT) * 4 + h
                    c = slice(idx, idx + 1)
                    cid = ("CDEC", idx)
                    pmj = [("PM", 2 * j), ("PM", 2 * j + 1)]
                    op("dve", lambda e, t0=t0, THx=THx: e.tensor_tensor_scan(rev(PM[:, t0:t0 + 128], 128), rev(THx[:, t0:t0 + 128], 128),
                                                                             zeros_ap(128), 1.0, ALU.mult, ALU.add),
                       reads=[thn, "ZER"], writes=pmj)
                    op("dve", lambda e, t0=t0, h=h, KKx=KKx: e.tensor_tensor(KE[:, h, t0:t0 + 127], KKx[:, t0:t0 + 127], PM[:, t0 + 1:t0 + 128], ALU.mult),
                       reads=[kkn] + pmj, writes=[("KE", h)])
                    op("dve", lambda e, t0=t0, h=h, KKx=KKx: e.tensor_copy(KE[:, h, t0 + 127:t0 + 128], KKx[:, t0 + 127:t0 + 128]),
                       reads=[kkn], writes=[("KE", h)])
                    op("dve", lambda e, t0=t0, c=c: e.tensor_copy(PLAST[:, c], PM[:, t0:t0 + 1]), reads=pmj, writes=[cid])

        def hgrn_tile_prev(pg_, j):
            lc = slice(j * 128, (j + 1) * 128)
            t = pg_ * TPG + j
            pv, pvid = big()
            for kc in range(8):
                sap, sidf = prev_buf(pg_ % 2, kc)
                op("pe", lambda e, kc=kc, pv=pv, sap=sap: e.matmul(pv[:], sap[:, lc], WIN[:, kc, 2048:2560], start=(kc == 0), stop=(kc == 7)),
                   reads=[("WIN", 4, kc), sidf(j)], writes=Q(pvid))
            pc = slice(t, t + 1)
            op("act", lambda e, pv=pv, pc=pc: e.activation(VT[:], pv[:], AF.Identity, scale=PMT[:, pc]),
               reads=Q(pvid) + ["PMT"], writes=["VT"])
            pt, ptid = ptb()
            for h in range(4):
                op("pe", lambda e, h=h, pt=pt: e.transpose(pt[:, h, :], KE[:, h, lc], IDB[:]),
                   reads=[("KE", h), "IDB"], writes=[(ptid, h)])
            op("dve", lambda e, pt=pt: e.tensor_copy(KET[:].rearrange("p (h c) -> p h c", h=4), pt[:, 0:4, :]),
               reads=[(ptid, h) for h in range(4)], writes=["KET"])
            for h in range(4):
                hs = slice(h * 128, (h + 1) * 128)
                op("pe", lambda e, hs=hs: e.matmul(PU[:, hs], KET[:, hs], VT[:, hs], start=True, stop=True),
                   reads=["KET", "VT"], writes=[("PU", h)])
            for h in range(4):
                hs = slice(h * 128, (h + 1) * 128)
                idx = (t % NT) * 4 + h
                c = slice(idx, idx + 1)
                op("dve", lambda e, hs=hs, c=c: e.scalar_tensor_tensor(SST[:, hs], SST[:, hs], PLAST[:, c], PU[:, hs], ALU.mult, ALU.add),
                   reads=[("S", h), ("CDEC", idx), ("PU", h)], writes=[("S", h)])

        def build_ht_prev(p, j, par):
            if p % 2 == 0:
                op("sp", lambda e: e.dma_start(out=GBC[:], in_=xprev_d[p * 128:(p + 1) * 128, :]),
                   writes=[("GBC", 0), ("GBC", 1)], dma="xprev")
                op("act", lambda e: e.activation(XN[:], GBC[:], AF.Identity, scale=RSP[:, p:p + 1]),
                   reads=[("GBC", 0), ("GBC", 1), ("RSP", p // 16)], writes=["XN"])
                XS, xsid = XN, "XN"
            else:
                op("sp", lambda e: e.dma_start(out=GV[:], in_=xprev_d[p * 128:(p + 1) * 128, 0:512]),
                   writes=Q("GV"), dma="xprevb")
                op("sp", lambda e: e.dma_start(out=TMP[:], in_=xprev_d[p * 128:(p + 1) * 128, 512:1024]),
                   writes=Q("TMP"), dma="xprevc")
                op("act", lambda e: e.activation(SQJ[:, 0:512], GV[:], AF.Identity, scale=RSP[:, p:p + 1]),
                   reads=Q("GV") + [("RSP", p // 16)], writes=["SQJ"])
                op("act", lambda e: e.activation(SQJ[:, 512:1024], TMP[:], AF.Identity, scale=RSP[:, p:p + 1]),
                   reads=Q("TMP") + [("RSP", p // 16)], writes=["SQJ"])
                XS, xsid = SQJ, "SQJ"
            pt, ptid = ptb()
            for kc in range(8):
                op("pe", lambda e, kc=kc: e.transpose(pt[:, kc, :], XS[:, kc * 128:(kc + 1) * 128], IDB[:]),
                   reads=[xsid, "IDB"], writes=[(ptid, kc)])
            for kc in range(8):
                bap, bid = prev_buf(par, kc)
                dst = bap[:, j * 128:(j + 1) * 128]
                wid = bid(j)
                op("act", lambda e, kc=kc, dst=dst: e.activation(dst, pt[:, kc, :], AF.Identity,
                                                                 bias=SH1C[:, kc:kc + 1], scale=A1C[:, kc:kc + 1]),
                   reads=[(ptid, kc), "A1C", "SH1C"], writes=[wid])

        if STAGE >= 2:
            npg = NPREV // TPG
            for j in range(TPG):
                build_ht_prev(j, j, 0)
            for pg_ in range(npg):
                hgrn_fm_prev(pg_, (0, 1))
                if pg_ + 1 < npg:
                    for j in range(TPG):
                        build_ht_prev((pg_ + 1) * TPG + j, j, (pg_ + 1) % 2)
                hgrn_fm_prev(pg_, (2, 3))
                for j in range(TPG):
                    hgrn_tile_prev(pg_, j)

        en[0] = STAGE >= 1.5
        to_wout(wada_rows(2), "wo")
        ada_bc(WOUT, lambda kc: [("WOUT", kc, 0), ("WOUT", kc, 1)], 0)
        to_wout(lambda kc: wout_d[kc * 128:(kc + 1) * 128, :], "wo")
        for kc in range(8):
            for hh in range(2):
                cs_ = slice(hh * 512, (hh + 1) * 512)
                op("pool", lambda e, kc=kc, cs_=cs_: e.tensor_tensor(WOUT[:, kc, cs_], WOUT[:, kc, cs_], GBC[:, cs_], ALU.mult),
                   reads=[("WOUT", kc, hh), ("GBC", hh)], writes=[("WOUT", kc, hh)])
        en[0] = True

        KG = int(os.environ.get('KG', str(NG)))
        for g in (range(KG) if STAGE >= 4 else []):
            for fc in range(4):
                pu, puid = big()
                for kc in range(8):
                    op("pe", lambda e, kc=kc, fc=fc, pu=pu, g=g: e.matmul(pu[:, 0:GT], WIN[:, kc, fc * 128:(fc + 1) * 128],
                                                                          HT[:, kc, gcols(g)], start=(kc == 0), stop=(kc == 7)),
                       reads=[("WIN", 0, kc)] + htg(g, kc), writes=Q(puid))
                op("act", lambda e, fc=fc, pu=pu: e.activation(GU[:, fc, :], pu[:, 0:GT], AF.Gelu), reads=Q(puid), writes=[("GU", fc)])
            if STAGE >= 4.1:
                hgrn_fm(g, not os.environ.get('KFM0'))
            for j in (range(TPG) if STAGE >= 4.2 else []):
                t = g * TPG + j
                lc = slice(j * 128, (j + 1) * 128)
                pv, pvid = big()
                for kc in range(8):
                    op("pe", lambda e, kc=kc, t=t, pv=pv: e.matmul(pv[:], HT[:, kc, t * 128:(t + 1) * 128], WIN[:, kc, 512:1024],
                                                                   start=(kc == 0), stop=(kc == 7)),
                       reads=[("WIN", 1, kc), ("HT", t, kc)], writes=Q(pvid))
                op("act", lambda e, pv=pv: e.activation(GV[:], pv[:], AF.Gelu), reads=Q(pvid), writes=Q("GV"))
                op("dve", lambda e: e.bn_stats(BNS[:], GV[:]), reads=Q("GV"), writes=["BNS"])
                op("dve", lambda e: e.bn_aggr(MV[:], BNS[:]), reads=["BNS"], writes=["MV"])
                op("dve", lambda e: e.tensor_scalar(NWV[:, 0:1], MV[:, 1:2], 1.0, EPS, ALU.mult, ALU.add),
                   reads=["MV"], writes=["NWV"])
                newton_rsqrt(NWV[:, 0:1], RSV[:], NWT[:, 0:1], "NWV", "RSV", "NWT")
                op("dve", lambda e: e.tensor_scalar(VH[:], GV[:], MV[:, 0:1], RSV[:], ALU.subtract, ALU.mult),
                   reads=Q("GV") + ["MV", "RSV"], writes=["VH"])
                pm, pmid = big()
                for h in range(4):
                    hs = slice(h * 128, (h + 1) * 128)
                    op("pe", lambda e, h=h, hs=hs, pm=pm: e.matmul(pm[:, hs], VH[:, hs], WST[:, h, :], start=True, stop=True),
                       reads=["VH", ("WST", h)], writes=[(pmid, h)])
                    op("dve", lambda e, h=h, hs=hs, pm=pm: e.scalar_tensor_tensor(TMP[:, hs], pm[:, hs], LNWC[:, h:h + 1],
                                                                                  BIAS2[:, h, :], ALU.mult, ALU.add),
                       reads=[(pmid, h), "LNWC", ("BIAS2", h)], writes=[("TMP", h)])
                    op(POOLC, lambda e, h=h, hs=hs, lc=lc: e.tensor_tensor(YT[:, h, lc], TMP[:, hs], GU[:, h, lc], ALU.mult),
                       reads=[("TMP", h), ("GU", h)], writes=[("YT", h, j)])
                en[0] = STAGE >= 4.3
                hgrn_tile(g, j, True)
                en[0] = STAGE >= 4.4
                pg, pgid = big()
                for kc in range(8):
                    op("pe", lambda e, kc=kc, t=t, pg=pg: e.matmul(pg[:], HT[:, kc, t * 128:(t + 1) * 128], WIN[:, kc, 2560:3072],
                                                                   start=(kc == 0), stop=(kc == 7)),
                       reads=[("WIN", 5, kc), ("HT", t, kc)], writes=Q(pgid))
                op("act", lambda e, pg=pg: e.activation(GV[:], pg[:], AF.Tanh, scale=0.5), reads=Q(pgid), writes=Q("GV"))
                op("dve", lambda e: e.tensor_scalar(GV[:], GV[:], 0.5, 0.5, ALU.mult, ALU.add), reads=Q("GV"), writes=Q("GV"))
                op("dve", lambda e, pg=pg: e.tensor_tensor(GV[:], pg[:], GV[:], ALU.mult), reads=Q(pgid) + Q("GV"), writes=Q("GV"))
                op(POOLC, lambda e: e.tensor_tensor(GV[:], GV[:], GNWBC[:], ALU.mult), reads=Q("GV") + ["GNWBC"], writes=Q("GV"))
                for h in range(4):
                    hs = slice(h * 128, (h + 1) * 128)
                    op("act", lambda e, h=h, hs=hs: e.activation(SQJ[:, hs], PO[:, hs], AF.Square, accum_out=SSO[:, h:h + 1]),
                       reads=[("PO", h)], writes=["SQJ", ("SSO", h)])
                op("dve", lambda e: e.tensor_scalar(NWV[:, 0:4], SSO[:], 1.0 / 128, EPS, ALU.mult, ALU.add),
                   reads=[("SSO", h) for h in range(4)], writes=["NWV"])
                newton_rsqrt(NWV[:, 0:4], RSO[:], NWT[:, 0:4], "NWV", "RSO", "NWT")
                for h in range(4):
                    hs = slice(h * 128, (h + 1) * 128)
                    op("dve", lambda e, h=h, hs=hs: e.scalar_tensor_tensor(YB[:, hs], PO[:, hs], RSO[:, h:h + 1], GV[:, hs],
                                                                           ALU.mult, ALU.mult),
                       reads=[("PO", h), "RSO", ("GV", h)], writes=[("YB", h)])
                pt, ptid = ptb()
                for h in range(4):
                    hs = slice(h * 128, (h + 1) * 128)
                    op("pe", lambda e, h=h, hs=hs, pt=pt: e.transpose(pt[:, 4 + h, :], YB[:, hs], IDB[:]),
                       reads=[("YB", h), "IDB"], writes=[(ptid, 4 + h)])
                op("act", lambda e, lc=lc, pt=pt: e.copy(YT[:, 4:8, lc], pt[:, 4:8, :]),
                   reads=[(ptid, 4 + h) for h in range(4)], writes=[("YT", 4 + h, j) for h in range(4)])
                en[0] = STAGE >= 4.5
                for half in range(2):
                    pw, pwid = big()
                    cs = slice(half * 512, (half + 1) * 512)
                    for kc in range(8):
                        op("pe", lambda e, kc=kc, cs=cs, lc=lc, pw=pw: e.matmul(pw[:], YT[:, kc, lc], WOUT[:, kc, cs],
                                                                                 start=(kc == 0), stop=(kc == 7)),
                           reads=[("YT", kc, j), ("WOUT", kc, half)], writes=Q(pwid))
                    op("dve", lambda e, cs=cs, pw=pw, t=t: e.tensor_tensor(X[:, t, cs], X[:, t, cs], pw[:], ALU.add),
                       reads=Q(pwid) + [("X", t)], writes=[("X", t)])
                en[0] = STAGE >= 4.6
                op("act", lambda e, t=t: e.activation(SQJ[:], X[:, t, :], AF.Square, accum_out=SS2[:]),
                   reads=[("X", t)], writes=["SQJ", "SS2"])
                op("dve", lambda e: e.tensor_scalar(NWV[:, 0:1], SS2[:], 1.0 / D, EPS, ALU.mult, ALU.add),
                   reads=["SS2"], writes=["NWV"])
                newton_rsqrt(NWV[:, 0:1], RS2[:], NWT[:, 0:1], "NWV", "RS2", "NWT")
                build_ht(t, RS2[:], "RS2", A2C, SH2C, "A2C", "SH2C")
                en[0] = True

        def do_dumps(names, base):
            en[0] = True
            dmap = {
                "GU0": (GU[:, 0, :], [("GU", 0)]), "KE0": (KE[:, 0, :], [("KE", 0)]), "QB0": (QB[:, 0, :], [("QB", 0)]),
                "PM": (PM[:], [("PM", b) for b in range(2 * TPG)]), "VT": (VT[:], ["VT"]), "GV": (GV[:], Q("GV")),
                "YB": (YB[:], [("YB", h) for h in range(4)]), "YTA": (YT[:, 0, :], [("YT", 0, j) for j in range(TPG)]),
                "YTB": (YT[:, 4, :], [("YT", 4, j) for j in range(TPG)]), "SST": (SST[:], [("S", h) for h in range(4)]),
                "X1": (X[:, 1, 0:512], [("X", 1)]), "HT1": (HT[:, 0, 128:256], [("HT", 1, 0)]),
                "HT0": (HT[:, 0, 0:128], [("HT", 0, 0)]), "VH": (VH[:], ["VH"]), "KET": (KET[:], ["KET"]),
                "ATM0": (ATM[:, 0, :], [("ATM", 0)]), "SBF": (SBF[:], [("SBF", h) for h in range(4)]),
                "C63": (C63[:], [("CDEC", i) for i in range(NT * 4)]), "C127": (C127[:], [("CDEC", i) for i in range(NT * 4)]),
                "RSO": (RSO[:], ["RSO"]), "GBC": (GBC[:, 0:512], [("GBC", 0)]), "COLS": (COLS[:], ["A1C", "SH1C", "A2C", "SH2C", "C0C", "C1C"]),
                "BIAS2": (BIAS2[:, 0, :], [("BIAS2", 0)]), "TH": (TH[:], ["TH"]), "KK": (KK[:], ["KK"]), "THQ": (THQ[:], ["THQ"]),
            }
            for i, nm in enumerate(names):
                ap, ids = dmap[nm]
                w = ap.shape[-1]
                op("dve", lambda e, ap=ap, w=w: e.tensor_copy(TMP[:, 0:w], ap), reads=ids, writes=Q("TMP"))
                op("sp", lambda e, i=i, w=w: e.dma_start(out=dbg_d[(base + i) * 128:(base + i + 1) * 128, 0:w], in_=TMP[:, 0:w]),
                   reads=Q("TMP"), writes=["dbgout"], dma="dbg")

        if DUMPS:
            do_dumps([d[4:] for d in DUMPS if d.startswith('pre_')], 0)
        if STAGE >= 5:
            to_wout(wada_rows(5), "wo")
            ada_bc(WOUT, lambda kc: [("WOUT", kc, 0), ("WOUT", kc, 1)], 1)

        bigs.extend([(PA, "PA"), (PO, "PO"), (PU, "PU")])
        ACTT = [GU, KE]
        ACTN = ["GU", "KE"]
        allwin = [("WIN", b, kc) for b in range(6) for kc in range(8)]
        WOF = WOUT[:].rearrange("p k c -> p (k c)")
        ACTB = [WOF[:, i * 2048:(i + 1) * 2048].rearrange("p (c t) -> p c t", c=4) for i in range(4)]
        if STAGE >= 5:
            op("pool", lambda e: e.memset(NWT2[:, 1:2], 0.0),
               writes=[("WOUT", kc, hh) for kc in range(8) for hh in range(2)] + ["WOUTFREE", ("NWT2", 0)])
        KP = int(os.environ.get('KP', '6'))
        for pi, (c0, n) in enumerate(FFN_PASSES[:KP] if STAGE >= 5 else []):
            slot = pi % 2
            base = slot * 12288
            WG = WINF[:, base:base + 4096].rearrange("p (k c) -> p k c", k=8)
            WU = WINF[:, base + 4096:base + 8192].rearrange("p (k c) -> p k c", k=8)
            WO = WINF[:, base + 8192:base + 12288].rearrange("p (c j) -> p c j", c=4)
            if pi == 0:
                op("pool", lambda e: e.memset(NWT2[:, 0:1], 0.0), writes=allwin + ["WINFREE", ("NWT2", 0)])
            extra = []
            gctr[0] += 1
            for kc in range(8):
                op("pool", lambda e, WG=WG, c0=c0, n=n, kc=kc: e.dma_start(
                    out=WG[:, kc, 0:n * 128], in_=wfi_d[kc * 128:(kc + 1) * 128, c0 * 128:(c0 + n) * 128]),
                   reads=["WINFREE"], writes=[("FWG", slot, kc)], dma="fwg%d" % slot, grp=gctr[0])
            for kc in range(8):
                op("pool", lambda e, WU=WU, c0=c0, n=n, kc=kc: e.dma_start(
                    out=WU[:, kc, 0:n * 128], in_=wfi_d[kc * 128:(kc + 1) * 128, DFF + c0 * 128:DFF + (c0 + n) * 128]),
                   reads=["WINFREE"], writes=[("FWU", slot, kc)], dma="fwu%d" % slot, grp=gctr[0])
            for ci in range(n):
                for hh in range(2):
                    op("pool", lambda e, WO=WO, c0=c0, ci=ci, hh=hh: e.dma_start(
                        out=WO[:, ci, hh * 512:(hh + 1) * 512],
                        in_=wfo_d[(c0 + ci) * 128:(c0 + ci + 1) * 128, hh * 512:(hh + 1) * 512]),
                       reads=["WINFREE"], writes=[("FWO", slot, ci, hh)], dma="fwo%d" % slot, grp=gctr[0])
            for ci in range(n):
                for hh in range(2):
                    cs_ = slice(hh * 512, (hh + 1) * 512)
                    op("pool", lambda e, WO=WO, ci=ci, cs_=cs_: e.tensor_tensor(WO[:, ci, cs_], WO[:, ci, cs_], GBC[:, cs_], ALU.mult),
                       reads=[("FWO", slot, ci, hh), ("GBC", hh)], writes=[("FWO", slot, ci, hh)])
            for gf in range(4):
                bi = (pi * 4 + gf) % 4
                at = ACTB[bi]
                hids = lambda kc, gf=gf: [("HT", gf * 4 + j, kc) for j in range(4)]
                for ci in range(n):
                    pg, pgid = big()
                    pu, puid = big()
                    for kc in range(8):
                        op("pe", lambda e, kc=kc, ci=ci, WG=WG, pg=pg, gf=gf: e.matmul(
                            pg[:], WG[:, kc, ci * 128:(ci + 1) * 128], HT[:, kc, gf * 512:(gf + 1) * 512], start=(kc == 0), stop=(kc == 7)),
                           reads=[("FWG", slot, kc)] + hids(kc), writes=Q(pgid))
                    for kc in range(8):
                        op("pe", lambda e, kc=kc, ci=ci, WU=WU, pu=pu, gf=gf: e.matmul(
                            pu[:], WU[:, kc, ci * 128:(ci + 1) * 128], HT[:, kc, gf * 512:(gf + 1) * 512], start=(kc == 0), stop=(kc == 7)),
                           reads=[("FWU", slot, kc)] + hids(kc), writes=Q(puid))
                    SIL, silid = (GV, Q("GV")) if ci % 2 == 0 else (TMP, Q("TMP"))
                    op("act", lambda e, pg=pg, SIL=SIL: e.activation(SIL[:], pg[:], AF.Silu), reads=Q(pgid), writes=silid)
                    op("dve", lambda e, ci=ci, at=at, pu=pu, SIL=SIL: e.tensor_tensor(at[:, ci, :], SIL[:], pu[:], ALU.mult),
                       reads=silid + Q(puid) + ["WOUTFREE"], writes=[("ACTT", bi, ci)])
                for j in range(4):
                    t = gf * 4 + j
                    lc = slice(j * 128, (j + 1) * 128)
                    for half in range(2):
                        cs = slice(half * 512, (half + 1) * 512)
                        po, poid = big()
                        for ci in range(n):
                            op("pe", lambda e, ci=ci, at=at, WO=WO, po=po, lc=lc, cs=cs, n=n: e.matmul(
                                po[:], at[:, ci, lc], WO[:, ci, cs], start=(ci == 0), stop=(ci == n - 1)),
                               reads=[("ACTT", bi, ci), ("FWO", slot, ci, half)], writes=Q(poid))
                        op("dve", lambda e, po=po, cs=cs, t=t: e.tensor_tensor(X[:, t, cs], X[:, t, cs], po[:], ALU.add),
                           reads=Q(poid) + [("X", t)], writes=[("X", t)])

        if DUMPS:
            do_dumps([d for d in DUMPS if not d.startswith('pre_')], len([d for d in DUMPS if d.startswith('pre_')]))
        en[0] = True
        load("sp", GV[:], fnw_d[:, 0:512], Q("GV"))
        load("sp", TMP[:], fnw_d[:, 512:1024], Q("TMP"))
        FNW = [GV, TMP]
        FNWID = [Q("GV"), Q("TMP")]
        for t in range(NT):
            op("act", lambda e, t=t: e.activation(SQJ[:], X[:, t, :], AF.Square, accum_out=SS1[:, t:t + 1]),
               reads=[("X", t)], writes=["SQJ", ("SS1", t)])
        op("dve", lambda e: e.tensor_scalar(NWV[:], SS1[:], 1.0 / D, EPS, ALU.mult, ALU.add),
           reads=[("SS1", t) for t in range(NT)], writes=["NWV"])
        newton_rsqrt(NWV[:], RS1[:], NWT[:], "NWV", "RS1", "NWT")
        for t in range(NT):
            for half in range(2):
                cs = slice(half * 512, (half + 1) * 512)
                op("dve", lambda e, t=t, cs=cs, half=half: e.scalar_tensor_tensor(X[:, t, cs], X[:, t, cs], RS1[:, t:t + 1],
                                                                                  FNW[half][:], ALU.mult, ALU.mult),
                   reads=[("X", t), "RS1"] + FNWID[half], writes=[("X", t)])
            op("sp", lambda e, t=t: e.dma_start(out=out_d[t * 128:(t + 1) * 128, :], in_=X[:, t, :]),
               reads=[("X", t)], dma="st%d" % (t % 4))

        S.emit(st)
    return nc


_NC_CACHE = {}


def _col(v, n):
    return np.ascontiguousarray(np.asarray(v, np.float32).reshape(n, 128).T)


def kernel(x, c, w_ada, b_ada, norm1_w, w_in, w_s, b_s, v_ln_w, v_ln_b,
           lower_bounds, gn_w, w_out, norm2_w, w_ffn_in, w_ffn_out, final_norm_w):
    f32 = lambda a: np.ascontiguousarray(np.asarray(a, dtype=np.float32))
    x = f32(x); c = f32(c)
    w_ada0 = f32(w_ada)[0]; b_ada0 = f32(b_ada)[0]
    w_in0 = f32(w_in)[0]; w_out0 = f32(w_out)[0]
    wfi0 = f32(w_ffn_in)[0]; wfo0 = f32(w_ffn_out)[0]
    w_s0 = f32(w_s)[0]; b_s0 = f32(b_s)[0]
    lb = f32(lower_bounds)
    if "nc" not in _NC_CACHE:
        _NC_CACHE["nc"] = build_nc()
    nc = _NC_CACHE["nc"]
    s_idx = np.arange(128)
    cid = s_idx // 64
    common = {
        "w_ada": w_ada0,
        "b_ada_col": _col(b_ada0, 48),
        "b_ada_bc": np.ascontiguousarray(np.broadcast_to(
            np.concatenate([b_ada0[2 * D:3 * D], b_ada0[5 * D:6 * D]])[None, :], (128, 2 * D))),
        "n1_col": _col(f32(norm1_w)[0], 8),
        "n2_col": _col(f32(norm2_w)[0], 8),
        "w_in": w_in0,
        "wsT": np.ascontiguousarray(w_s0.transpose(2, 0, 1).reshape(128, 512)),
        "bs_bc": np.ascontiguousarray(np.broadcast_to(b_s0.reshape(1, 512), (128, 512))),
        "lnw_col": _col(f32(v_ln_w)[0], 4),
        "lnb_col": _col(f32(v_ln_b)[0], 4),
        "lba0": _col(lb[0], 4),
        "lba1": _col(lb[1], 4),
        "gnw_bc": np.ascontiguousarray(np.broadcast_to(np.tile(f32(gn_w)[0], 4)[None, :], (128, 512))),
        "w_out": w_out0,
        "w_ffn_in": wfi0,
        "w_ffn_out": wfo0,
        "fnw_bc": np.ascontiguousarray(np.broadcast_to(f32(final_norm_w)[None, :], (128, D))),
        "ident": np.eye(128, dtype=np.float32).astype(ml_dtypes.bfloat16),
        "cmask": (s_idx[:, None] <= s_idx[None, :]).astype(np.int32),
        "gmask": (cid[None, :] >= cid[:, None]).astype(np.float32),
    }
    in_maps = []
    for r in range(NCORES):
        b, seg = r // 4, r % 4
        pm = np.zeros((128, 8), np.float32)
        for j in range(NCORES):
            if j // 4 == b and j < r:
                pm[:, j] = 1.0
        m = dict(common)
        m["x"] = np.ascontiguousarray(x[b, seg * TOK:(seg + 1) * TOK, :])
        m["c_col"] = _col(c[b], 8)
        m["pmask"] = pm
        xp = np.zeros((NPREV * 128, D), np.float32)
        pmt = np.zeros((128, NPREV), np.float32)
        npred = seg * TOK
        if npred:
            xp[NPREV * 128 - npred:] = x[b, 0:npred, :]
            pmt[:, NPREV - npred // 128:] = 1.0
        m["xprev"] = xp
        m["pmt"] = pmt
        in_maps.append(m)
    res = run_bass_kernel_spmd(nc, in_maps, core_ids=list(range(NCORES)))
    out = np.empty((2, SEQ, D), np.float32)
    if 'dbg' in res.results[0]:
        _NC_CACHE['dbg'] = [np.asarray(r['dbg']) for r in res.results]
    for r in range(NCORES):
        b, seg = r // 4, r % 4
        out[b, seg * TOK:(seg + 1) * TOK, :] = np.asarray(res.results[r]["out"], dtype=np.float32)
    return out
```

```python
from contextlib import ExitStack
import numpy as np
import ml_dtypes
import concourse.bass as bass
import concourse.mybir as mybir
from concourse.bass_utils import run_bass_kernel_spmd

F32 = mybir.dt.float32
BF16 = mybir.dt.bfloat16
I32 = mybir.dt.int32
AF = mybir.ActivationFunctionType
ALU = mybir.AluOpType

import os
STAGE = float(os.environ.get('KSTAGE', '9'))
SERIAL = int(os.environ.get('KSERIAL', '0'))
REORDER = int(os.environ.get('KREORDER', '1'))
NCORES = 8
D = 1024
SEQ = 8192
TOK = 2048
NT = 16
NG = 8
GT = 256
TPG = 2
DFF = 2816
NCH = 22
NEWTON_ITERS = 2
NPREV = 48
EPS = 1e-6
FFN_PASSES = [(0, 4), (4, 4), (8, 4), (12, 4), (16, 3), (19, 3)]


class _Op:
    __slots__ = ("eng", "fn", "reads", "writes", "dma", "idx", "pos", "deps",
                 "signal", "sigval", "dcount", "grp", "gcount", "inc")

    def __init__(self, eng, fn, reads, writes, dma):
        self.eng, self.fn, self.reads, self.writes, self.dma = eng, fn, reads, writes, dma
        self.deps = []
        self.signal = False
        self.sigval = None
        self.dcount = None


class Sched:
    ENGS = ("pe", "act", "dve", "pool", "sp")
    PSUM_BANKS = ("PB0", "PB1", "PB2", "PT0", "PT1", "PA", "PO", "PU")

    def __init__(self, nc):
        self.nc = nc
        self.ops = []

    def op(self, eng, fn, reads=(), writes=(), dma=None, grp=None, inc=16):
        o = _Op(eng, fn, tuple(reads), tuple(writes), dma)
        o.inc = inc
        o.grp = grp
        o.gcount = None
        o.idx = len(self.ops)
        self.ops.append(o)
        return o

    COST = {"pe": 0.17, "act": 0.47, "dve": 0.41, "pool": 0.7, "sp": 0.05}

    def reorder(self):
        import heapq
        ops = self.ops
        n = len(ops)
        last_w, readers = {}, {}
        preds = [set() for _ in range(n)]
        last_key = {}
        for o in ops:
            i = o.idx
            if o.dma is not None:
                if o.dma in last_key:
                    preds[i].add(last_key[o.dma])
                last_key[o.dma] = i
            elif o.eng == "pe":
                for b in o.writes:
                    if isinstance(b, tuple) and b[0] in self.PSUM_BANKS:
                        k = ("pebank", b[0])
                        if k in last_key and last_key[k] != i:
                            preds[i].add(last_key[k])
                        last_key[k] = i
            for b in o.reads:
                if b in last_w:
                    preds[i].add(last_w[b])
            for b in o.writes:
                if b in last_w:
                    preds[i].add(last_w[b])
                for r in readers.get(b, ()):
                    preds[i].add(r)
            for b in o.reads:
                readers.setdefault(b, []).append(i)
            for b in o.writes:
                last_w[b] = i
                readers[b] = []
            preds[i].discard(i)
        members = {}
        for o in ops:
            if o.dma is not None and o.grp is not None:
                members.setdefault((o.dma, o.grp), []).append(o.idx)
        for i in range(n):
            extra = set()
            for j in preds[i]:
                p = ops[j]
                if p.dma is not None and p.grp is not None and not (ops[i].dma == p.dma and ops[i].grp == p.grp):
                    extra.update(members[(p.dma, p.grp)])
            extra.discard(i)
            preds[i] |= extra
        succs = [[] for _ in range(n)]
        indeg = [0] * n
        for i in range(n):
            indeg[i] = len(preds[i])
            for j in preds[i]:
                succs[j].append(i)
        def cost_of(o):
            return 3.0 if o.dma is not None else self.COST[o.eng] + 0.15
        blevel = [0.0] * n
        for i in range(n - 1, -1, -1):
            m = 0.0
            for k in succs[i]:
                if blevel[k] > m:
                    m = blevel[k]
            blevel[i] = m + cost_of(ops[i])
        finish = [0.0] * n
        ready_t = [0.0] * n
        eng_free = {e: 0.0 for e in self.ENGS}
        ready = {e: [] for e in self.ENGS}
        for i in range(n):
            if indeg[i] == 0:
                ready[ops[i].eng].append(i)
        order = []
        done = 0
        while done < n:
            best = None
            for e in self.ENGS:
                if not ready[e]:
                    continue
                tmin = min(ready_t[i] for i in ready[e])
                st = max(tmin, eng_free[e])
                if best is None or st < best[0]:
                    best = (st, e)
            st, e = best
            cands = [i for i in ready[e] if ready_t[i] <= st + 1e-9]
            i = max(cands, key=lambda i: (blevel[i], -i))
            ready[e].remove(i)
            o = ops[i]
            if o.dma is not None:
                eng_free[e] = st + 0.05
                finish[i] = st + 3.0
            else:
                eng_free[e] = st + self.COST[e]
                finish[i] = eng_free[e] + 0.15
            order.append(i)
            done += 1
            for k in succs[i]:
                ready_t[k] = max(ready_t[k], finish[i])
                indeg[k] -= 1
                if indeg[k] == 0:
                    ready[ops[k].eng].append(k)
        self.ops = [ops[i] for i in order]
        for k, o in enumerate(self.ops):
            o.idx = k
        self.sim_span = max(finish)

    def emit(self, stack):
        if REORDER:
            self.reorder()
        nc = self.nc
        ops = self.ops
        last_w = {}
        readers = {}
        pos_ctr = {e: 0 for e in self.ENGS}
        waited = {e: {p: 0 for p in self.ENGS} for e in self.ENGS}
        waited_dma = {e: {} for e in self.ENGS}
        dma_ctr = {}
        gmax = {}
        for o in ops:
            if o.dma is not None:
                dma_ctr[o.dma] = dma_ctr.get(o.dma, 0) + 1
                o.dcount = dma_ctr[o.dma]
                if o.grp is not None:
                    gmax[(o.dma, o.grp)] = o.dcount
        for o in ops:
            if o.dma is not None:
                o.gcount = gmax[(o.dma, o.grp)] if o.grp is not None else o.dcount
        bank_last = {b: {} for b in self.PSUM_BANKS}
        for o in ops:
            if o.dma is None:
                pos_ctr[o.eng] += 1
                o.pos = pos_ctr[o.eng]
            raw = set()
            war = set()
            if SERIAL and o.idx > 0:
                raw.add(o.idx - 1)
            banks = set()
            for b in o.reads + o.writes:
                if isinstance(b, tuple) and b[0] in bank_last:
                    banks.add(b[0])
            for bk in banks:
                for e2, j2 in bank_last[bk].items():
                    if e2 != o.eng:
                        raw.add(j2)
                bank_last[bk][o.eng] = o.idx
            for b in o.reads:
                if b in last_w:
                    raw.add(last_w[b])
            for b in o.writes:
                if b in last_w:
                    raw.add(last_w[b])
                for r in readers.get(b, ()):
                    war.add(r)
            deps = []
            for j in sorted(raw | war):
                p = ops[j]
                if j == o.idx:
                    continue
                if p.dma is not None:
                    k = p.dma
                    if o.dma == p.dma and o.grp is not None and o.grp == p.grp:
                        continue
                    if waited_dma[o.eng].get(k, 0) >= p.gcount:
                        continue
                    waited_dma[o.eng][k] = p.gcount
                    deps.append(j)
                    continue
                if p.eng == o.eng and o.dma is None:
                    if o.eng == "pe":
                        continue
                if waited[o.eng][p.eng] >= p.pos:
                    continue
                waited[o.eng][p.eng] = p.pos
                p.signal = True
                deps.append(j)
            o.deps = deps
            for b in o.reads:
                readers.setdefault(b, []).append(o.idx)
            for b in o.writes:
                last_w[b] = o.idx
                readers[b] = []
        consumed = set()
        for o in ops:
            consumed.update(o.deps)
        tail_dmas = [o for o in ops if o.dma is not None and o.idx not in consumed]
        sig_ctr = {e: 0 for e in self.ENGS}
        for o in ops:
            if o.dma is None and o.signal:
                sig_ctr[o.eng] += 1
                o.sigval = sig_ctr[o.eng]
        sems = {e: stack.enter_context(nc.semaphore("s_" + e)) for e in self.ENGS}
        dsems = {k: stack.enter_context(nc.semaphore("d_" + str(k))) for k in dma_ctr}
        engobj = {"pe": nc.tensor, "act": nc.scalar, "dve": nc.vector,
                  "pool": nc.gpsimd, "sp": nc.sync}
        for o in ops:
            e = engobj[o.eng]
            for j in o.deps:
                p = ops[j]
                if p.dma is not None:
                    e.wait_ge(dsems[p.dma], p.inc * p.gcount)
                else:
                    e.wait_ge(sems[p.eng], p.sigval)
            ins = o.fn(e)
            if o.dma is not None:
                ins.then_inc(dsems[o.dma], o.inc)
            elif o.signal:
                ins.then_inc(sems[o.eng], 1)
        done = set()
        for o in tail_dmas:
            if o.dma in done:
                continue
            done.add(o.dma)
            nc.sync.wait_ge(dsems[o.dma], 16 * dma_ctr[o.dma])


def Q(name):
    return [(name, q) for q in range(4)]


def build_nc():
    nc = bass.Bass("TRN2", target_bir_lowering=False)

    def din(name, shape, dt=F32):
        return nc.dram_tensor(name, list(shape), dt, kind="ExternalInput")

    x_d = din("x", [TOK, D])
    ccol_d = din("c_col", [128, 8])
    wada_d = din("w_ada", [D, 6 * D])
    badac_d = din("b_ada_col", [128, 48])
    badab_d = din("b_ada_bc", [128, 2 * D])
    n1c_d = din("n1_col", [128, 8])
    n2c_d = din("n2_col", [128, 8])
    win_d = din("w_in", [D, 3072])
    wst_d = din("wsT", [128, 512])
    bsbc_d = din("bs_bc", [128, 512])
    lnw_d = din("lnw_col", [128, 4])
    lnb_d = din("lnb_col", [128, 4])
    lba0_d = din("lba0", [128, 4])
    lba1_d = din("lba1", [128, 4])
    gnw_d = din("gnw_bc", [128, 512])
    wout_d = din("w_out", [D, D])
    wfi_d = din("w_ffn_in", [D, 2 * DFF])
    wfo_d = din("w_ffn_out", [DFF, D])
    fnw_d = din("fnw_bc", [128, D])
    ident_d = din("ident", [128, 128], BF16)
    cmask_d = din("cmask", [128, 128], I32)
    gmask_d = din("gmask", [128, 128])
    pmask_d = din("pmask", [128, 8])
    xprev_d = din("xprev", [NPREV * 128, D])
    pmt_d = din("pmt", [128, NPREV])
    out_d = nc.dram_tensor("out", [TOK, D], F32, kind="ExternalOutput")
    DUMPS = [d for d in os.environ.get('KDUMP', '').split(',') if d]
    dbg_d = nc.dram_tensor("dbg", [max(1, len(DUMPS)) * 128, 512], F32, kind="ExternalOutput") if DUMPS else None
    ccin_d = nc.dram_tensor("cc_in", [128, 516], F32)
    ccout_d = nc.dram_tensor("cc_out", [NCORES * 128, 516], F32)

    S = Sched(nc)
    en = [True]
    POOLC = os.environ.get('KPOOL', 'pool')

    def op(*a, **k):
        if en[0]:
            return S.op(*a, **k)
    with ExitStack() as st:
        def sb(name, shape, dt=F32):
            return st.enter_context(nc.sbuf_tensor(name, list(shape), dt))

        def ps(name, shape, dt=F32):
            return st.enter_context(nc.psum_tensor(name, list(shape), dt))

        X = sb("X", [128, NT, D])
        HT = sb("HT", [128, 8, TOK], BF16)
        WINF = sb("WINF", [128, 8 * 3072], BF16)
        WIN = WINF[:].rearrange("p (k c) -> p k c", k=8)
        WOUT = sb("WOUT", [128, 8, D], BF16)
        IDB = sb("IDB", [128, 128], BF16)
        ONESB = sb("ONESB", [128, 128], BF16)
        CMASK = sb("CMASK", [128, 128], I32)
        GMASK = sb("GMASK", [128, 128])
        PMT = sb("PMT", [128, NPREV])
        WST = sb("WST", [128, 4, 128], BF16)
        BIAS2 = sb("BIAS2", [128, 4, 128])
        GNWBC = sb("GNWBC", [128, 512])
        GBC = sb("GBC", [128, D])
        COLS = sb("COLS", [128, 128])
        col_ctr = [0]

        def cols(n):
            a = col_ctr[0]
            col_ctr[0] += n
            assert col_ctr[0] <= 128
            return COLS[:, a:a + n]
        CCOL = cols(8); CTH = cols(8); N1C = cols(8); N2C = cols(8)
        A1C = cols(8); SH1C = cols(8); A2C = cols(8); SH2C = cols(8)
        SC1C = cols(8); SC2C = cols(8)
        LNWC = cols(4); LNBC = cols(4); LBA0 = cols(4); LBA1 = cols(4)
        LBC = cols(4); C0C = cols(4); C1C = cols(4); NC1C = cols(4)
        DT = cols(4); DMC = cols(4)
        BADAC = sb("BADAC", [128, 48])
        CACTB = sb("CACTB", [128, 8], BF16)
        CREP = sb("CREP", [128, 8, 128], BF16)
        C63 = sb("C63", [128, NT * 4]); C127 = sb("C127", [128, NT * 4])
        R1 = sb("R1", [128, NT * 4]); PLAST = sb("PLAST", [128, NT * 4])
        SS1 = sb("SS1", [128, NT]); RS1 = sb("RS1", [128, NT])
        NWV = sb("NWV", [128, 16]); NWT = sb("NWT", [128, 16])
        SSO = sb("SSO", [128, 4]); RSO = sb("RSO", [128, 4])
        SS2 = sb("SS2", [128, 1]); RS2 = sb("RS2", [128, 1])
        BNS = sb("BNS", [128, 6]); MV = sb("MV", [128, 2]); RSV = sb("RSV", [128, 1])
        SST = sb("SST", [128, 512])
        SBF = sb("SBF", [128, 512], BF16)
        SSP = sb("SSP", [128, NPREV]); RSP = sb("RSP", [128, NPREV])
        NWV2 = SSP; NWT2 = sb("NWT2", [128, NPREV])
        XN = sb("XN", [128, D], BF16)
        SQJ = sb("SQJ", [128, D], BF16)
        GU = sb("GU", [128, 4, GT], BF16)
        KE = sb("KE", [128, 4, GT], BF16)
        QB = sb("QB", [128, 4, GT], BF16)
        YT = sb("YT", [128, 8, GT], BF16)
        TH = sb("TH", [128, GT]); KK = sb("KK", [128, GT]); PM = sb("PM", [128, GT])
        TH2 = sb("TH2", [128, GT]); KK2 = sb("KK2", [128, GT])
        THQ = sb("THQ", [128, GT])
        GV = sb("GV", [128, 512])
        TMP = sb("TMP", [128, 512])
        VH = sb("VH", [128, 512], BF16); VT = sb("VT", [128, 512], BF16)
        YB = sb("YB", [128, 512], BF16)
        KET = sb("KET", [128, 512], BF16); ATM = sb("ATM", [128, 4, 128], BF16)
        ZER = sb("ZER", [128, 64])
        PB = [ps("PB%d" % i, [128, 512]) for i in range(3)]
        PT = [ps("PT%d" % i, [128, 8, 128], BF16) for i in range(2)]
        PA = ps("PA", [128, 512]); PO = ps("PO", [128, 512]); PU = ps("PU", [128, 512])
        bigs = [(PB[0], "PB0"), (PB[1], "PB1"), (PB[2], "PB2")]
        big_ctr = [0]

        def big():
            b = bigs[big_ctr[0] % len(bigs)]
            big_ctr[0] += 1
            return b
        pt_ctr = [0]

        def ptb():
            i = pt_ctr[0] % 2
            pt_ctr[0] += 1
            return PT[i], "PT%d" % i
        dk = [0]

        def dkey(prefix):
            dk[0] += 1
            return "%s%d" % (prefix, dk[0])

        def htr(t):
            return [("HT", t, kc) for kc in range(8)]

        def htg(g, kc):
            return [("HT", g * TPG + j, kc) for j in range(TPG)]

        def newton_rsqrt(v, y, t, vid, yid, tid):
            vi = v.bitcast(I32); yi = y.bitcast(I32)
            op("dve", lambda e: e.tensor_scalar(yi, vi, 1, None, ALU.arith_shift_right),
               reads=[vid], writes=[yid])
            op("dve", lambda e: e.tensor_scalar(yi, yi, -1, 0x5f3759df, ALU.mult, ALU.add),
               reads=[yid], writes=[yid])
            for _ in range(NEWTON_ITERS):
                op("dve", lambda e: e.tensor_tensor(t, y, y, ALU.mult), reads=[yid], writes=[tid])
                op("dve", lambda e: e.tensor_tensor(t, t, v, ALU.mult), reads=[tid, vid], writes=[tid])
                op("dve", lambda e: e.tensor_scalar(t, t, -0.5, 1.5, ALU.mult, ALU.add),
                   reads=[tid], writes=[tid])
                op("dve", lambda e: e.tensor_tensor(y, y, t, ALU.mult), reads=[tid, yid], writes=[yid])

        def load(eng, dst, src, wid, prefix="l"):
            op(eng, lambda e: e.dma_start(out=dst, in_=src), writes=wid, dma=dkey(prefix))

        en[0] = STAGE >= 0.3
        load("sp", CCOL, ccol_d[:], ["CCOL"])
        load("sp", N1C, n1c_d[:], ["N1C"])
        load("sp", N2C, n2c_d[:], ["N2C"])
        load("sp", LNWC, lnw_d[:], ["LNWC"])
        load("sp", LNBC, lnb_d[:], ["LNBC"])
        load("sp", LBA0, lba0_d[:], ["LBA0"])
        load("sp", LBA1, lba1_d[:], ["LBA1"])
        load("sp", BADAC[:], badac_d[:], ["BADAC"])
        load("sp", IDB[:], ident_d[:], ["IDB"])
        load("sp", CMASK[:], cmask_d[:], ["CMASK"])
        load("sp", GMASK[:], gmask_d[:], ["GMASK"])
        load("sp", PMT[:], pmt_d[:], ["PMT"])
        load("sp", GNWBC[:], gnw_d[:], ["GNWBC"])
        op("dve", lambda e: e.tensor_scalar(GNWBC[:], GNWBC[:], 0.5, None, ALU.mult), reads=["GNWBC"], writes=["GNWBC"])
        en[0] = STAGE >= 0.5
        WADA_HT = [HT[:, :, 0:1024], HT[:, :, 1024:2048]]
        HT_IDS = [[("HT", t, kc) for t in range(0, 8) for kc in range(8)],
                  [("HT", t, kc) for t in range(8, 16) for kc in range(8)]]

        gctr = [0]

        def wada_to_ht(blk, half):
            gctr[0] += 1
            for kc in range(8):
                for hh in range(2):
                    op("pool", lambda e, kc=kc, hh=hh: e.dma_start(
                        out=HT[:, kc, half * 1024 + hh * 512:half * 1024 + (hh + 1) * 512],
                        in_=wada_d[kc * 128:(kc + 1) * 128, blk * D + hh * 512:blk * D + (hh + 1) * 512]),
                       writes=[("HT", t, kc) for t in range(half * 8 + hh * 4, half * 8 + hh * 4 + 4)],
                       dma="wada_ht%d" % half, grp=gctr[0])

        def to_wout(src_ap_fn, key):
            gctr[0] += 1
            for kc in range(8):
                for hh in range(2):
                    op("pool", lambda e, kc=kc, hh=hh: e.dma_start(out=WOUT[:, kc, hh * 512:(hh + 1) * 512],
                                                                   in_=src_ap_fn(kc)[:, hh * 512:(hh + 1) * 512]),
                       reads=[("RSP", 0)], writes=[("WOUT", kc, hh)], dma=key, grp=gctr[0])

        def wada_rows(blk):
            return lambda kc: wada_d[kc * 128:(kc + 1) * 128, blk * D:(blk + 1) * D]
        wada_to_ht(0, 0)
        wada_to_ht(1, 1)
        en[0] = STAGE >= 2
        for bq in range(NPREV // 16):
            bs = slice(bq * 16, (bq + 1) * 16)
            for p in range(bq * 16, (bq + 1) * 16):
                op("sp", lambda e, p=p: e.dma_start(out=X[:, p % NT, :], in_=xprev_d[p * 128:(p + 1) * 128, :]),
                   writes=[("X", p % NT)], dma="xprev0_%d" % (p % NT))
                op("act", lambda e, p=p: e.activation(XN[:], X[:, p % NT, :], AF.Square, accum_out=SSP[:, p:p + 1]),
                   reads=[("X", p % NT)], writes=["XN", ("SSP", p)])
            op("dve", lambda e, bs=bs: e.tensor_scalar(NWV2[:, bs], SSP[:, bs], 1.0 / D, EPS, ALU.mult, ALU.add),
               reads=[("SSP", p) for p in range(bq * 16, (bq + 1) * 16)], writes=[("NWV2", bq)])
            newton_rsqrt(NWV2[:, bs], RSP[:, bs], NWT2[:, bs], ("NWV2", bq), ("RSP", bq), ("NWT2", bq))
        en[0] = True
        for t in range(NT):
            load("act", X[:, t, :], x_d[t * 128:(t + 1) * 128, :], [("X", t)], "x")
        en[0] = STAGE >= 0.3

        def win_load(blk):
            gctr[0] += 1
            late = [("RSP", 0)] if blk not in (3, 4) else []
            for kc in range(8):
                op("pool", lambda e, kc=kc: e.dma_start(out=WIN[:, kc, blk * 512:(blk + 1) * 512],
                                                        in_=win_d[kc * 128:(kc + 1) * 128, blk * 512:(blk + 1) * 512]),
                   reads=late, writes=[("WIN", blk, kc)], dma="win%d" % blk, grp=gctr[0])

        op("pool", lambda e: e.memset(ONESB[:], 1.0), writes=["ONESB"])
        op("pool", lambda e: e.memset(ZER[:], 0.0), writes=["ZER"])
        op("pool", lambda e: e.memset(ATM[:], 0.0), writes=[("ATM", h) for h in range(4)])
        op("pool", lambda e: e.memset(SST[:], 0.0), writes=[("S", h) for h in range(4)])
        op("pool", lambda e: e.memset(DT, 1.0), writes=["DT"])

        op("act", lambda e: e.activation(CTH, CCOL, AF.Tanh, scale=0.5), reads=["CCOL"], writes=["CTH"])
        op("dve", lambda e: e.tensor_scalar(CTH, CTH, 0.5, 0.5, ALU.mult, ALU.add), reads=["CTH"], writes=["CTH"])
        op("dve", lambda e: e.tensor_tensor(CACTB[:], CCOL, CTH, ALU.mult), reads=["CCOL", "CTH"], writes=["CACTB"])
        for kc in range(8):
            op("dve", lambda e, kc=kc: e.tensor_scalar(CREP[:, kc, :], ONESB[:], CTH[:, kc:kc + 1], CCOL[:, kc:kc + 1],
                                                        ALU.mult, ALU.mult),
               reads=["ONESB", "CTH", "CCOL"], writes=[("CREP", kc)])
        op("dve", lambda e: e.tensor_tensor(LBC, LBA0, LBA1, ALU.subtract), reads=["LBA0", "LBA1"], writes=["LBC"])
        op("act", lambda e: e.activation(LBC, LBC, AF.Tanh, scale=0.5), reads=["LBC"], writes=["LBC"])
        op("dve", lambda e: e.tensor_scalar(LBC, LBC, 0.5, 0.5, ALU.mult, ALU.add), reads=["LBC"], writes=["LBC"])
        op("dve", lambda e: e.tensor_scalar(C1C, LBC, -0.5, 0.5, ALU.mult, ALU.add), reads=["LBC"], writes=["C1C"])
        op("dve", lambda e: e.tensor_scalar(C0C, LBC, 0.5, 0.5, ALU.mult, ALU.add), reads=["LBC"], writes=["C0C"])
        op("dve", lambda e: e.tensor_scalar(NC1C, C1C, -1.0, None, ALU.mult), reads=["C1C"], writes=["NC1C"])

        en[0] = STAGE >= 0.5
        def ada_cols(stage, sid, gi, dst, dstid):
            if os.environ.get('KNOADA'):
                return
            for half in range(2):
                pb, pid = big()
                for q in range(4):
                    fc = half * 4 + q
                    for kc in range(8):
                        op("pe", lambda e, fc=fc, kc=kc, q=q, pb=pb: e.matmul(pb[:, q * 128:(q + 1) * 128],
                                                                             stage[:, kc, fc * 128:(fc + 1) * 128],
                                                                             CREP[:, kc, :], start=(kc == 0), stop=(kc == 7)),
                           reads=list(sid(kc)) + [("CREP", kc)], writes=[(pid, q)])
                op("dve", lambda e, pb=pb: e.tensor_copy(TMP[:], pb[:]), reads=Q(pid), writes=Q("TMP"))
                for q in range(4):
                    fc = half * 4 + q
                    op("dve", lambda e, fc=fc, q=q: e.tensor_tensor(dst[:, fc:fc + 1], TMP[:, q * 128:q * 128 + 1],
                                                                    BADAC[:, gi * 8 + fc:gi * 8 + fc + 1], ALU.add),
                       reads=[("TMP", q), "BADAC"], writes=[dstid])

        def ada_bc(stage, sid, which):
            for half in range(2):
                load("sp", TMP[:], badab_d[:, which * D + half * 512: which * D + (half + 1) * 512], Q("TMP"))
                pb, pid = big()
                for kc in range(8):
                    op("pe", lambda e, kc=kc, half=half, pb=pb: e.matmul(pb[:], CREP[:, kc, :],
                                                                         stage[:, kc, half * 512:(half + 1) * 512],
                                                                         start=(kc == 0), stop=(kc == 7)),
                       reads=list(sid(kc)) + [("CREP", kc)], writes=Q(pid))
                op("dve", lambda e, half=half, pb=pb: e.tensor_tensor(GBC[:, half * 512:(half + 1) * 512], pb[:], TMP[:], ALU.add),
                   reads=Q(pid) + Q("TMP"), writes=[("GBC", half)])

        ada_cols(WADA_HT[0], lambda kc: [("HT", t, kc) for t in range(0, 8)], 0, SH1C, "SH1C")
        ada_cols(WADA_HT[1], lambda kc: [("HT", t, kc) for t in range(8, 16)], 1, SC1C, "SC1C")
        op("dve", lambda e: e.scalar_tensor_tensor(A1C, SC1C, 1.0, N1C, ALU.add, ALU.mult),
           reads=["SC1C", "N1C"], writes=["A1C"])
        win_load(3)
        win_load(4)

        en[0] = STAGE >= 0.7
        for t in range(NT):
            op("act", lambda e, t=t: e.activation(SQJ[:], X[:, t, :], AF.Square, accum_out=SS1[:, t:t + 1]),
               reads=[("X", t)], writes=["SQJ", ("SS1", t)])
        op("dve", lambda e: e.tensor_scalar(NWV[:], SS1[:], 1.0 / D, EPS, ALU.mult, ALU.add),
           reads=[("SS1", t) for t in range(NT)], writes=["NWV"])
        newton_rsqrt(NWV[:], RS1[:], NWT[:], "NWV", "RS1", "NWT")

        KHT = int(os.environ.get("KHT", "9"))

        def build_ht(t, rs_col, rsid, AC, SHC, acid, shid):
            op("act", lambda e: e.activation(XN[:], X[:, t, :], AF.Identity, scale=rs_col),
               reads=[("X", t), rsid], writes=["XN"])
            if KHT < 2:
                return
            pt, ptid = ptb()
            for kc in range(8):
                op("pe", lambda e, kc=kc: e.transpose(pt[:, kc, :], XN[:, kc * 128:(kc + 1) * 128], IDB[:]),
                   reads=["XN", "IDB"], writes=[(ptid, kc)])
            if KHT < 3:
                return
            for kc in range(8):
                dst = HT[:, kc, t * 128:(t + 1) * 128]
                if True:
                    op("act", lambda e, kc=kc, dst=dst: e.activation(dst, pt[:, kc, :], AF.Identity,
                                                                     bias=SHC[:, kc:kc + 1], scale=AC[:, kc:kc + 1]),
                       reads=[(ptid, kc), acid, shid], writes=[("HT", t, kc)])
                else:
                    op("dve", lambda e, kc=kc, dst=dst: e.tensor_scalar(dst, pt[:, kc, :], AC[:, kc:kc + 1],
                                                                        SHC[:, kc:kc + 1], ALU.mult, ALU.add),
                       reads=[(ptid, kc), acid, shid], writes=[("HT", t, kc)])

        en[0] = STAGE >= 1
        for t in range(NT):
            build_ht(t, RS1[:, t:t + 1], "RS1", A1C, SH1C, "A1C", "SH1C")

        en[0] = STAGE >= 1.5
        to_wout(wada_rows(3), "wo")
        ada_cols(WOUT, lambda kc: [("WOUT", kc, 0), ("WOUT", kc, 1)], 3, SH2C, "SH2C")
        to_wout(wada_rows(4), "wo")
        ada_cols(WOUT, lambda kc: [("WOUT", kc, 0), ("WOUT", kc, 1)], 4, SC2C, "SC2C")
        op("dve", lambda e: e.scalar_tensor_tensor(A2C, SC2C, 1.0, N2C, ALU.add, ALU.mult),
           reads=["SC2C", "N2C"], writes=["A2C"])
        for blk in (0, 1, 2, 5):
            win_load(blk)

        load("sp", GV[:], wst_d[:], Q("GV"))
        load("sp", TMP[:], bsbc_d[:], Q("TMP"))
        for h in range(4):
            hs = slice(h * 128, (h + 1) * 128)
            op("dve", lambda e, h=h, hs=hs: e.tensor_tensor(WST[:, h, :], GV[:, hs], GMASK[:], ALU.mult),
               reads=[("GV", h), "GMASK"], writes=[("WST", h)])
        pb, pid = big()
        for h in range(4):
            hs = slice(h * 128, (h + 1) * 128)
            op("pe", lambda e, h=h, hs=hs, pb=pb: e.matmul(pb[:, hs], ONESB[:], WST[:, h, :], start=True, stop=True),
               reads=["ONESB", ("WST", h)], writes=[(pid, h)])
            op("dve", lambda e, h=h, hs=hs, pb=pb: e.scalar_tensor_tensor(BIAS2[:, h, :], pb[:, hs], LNBC[:, h:h + 1],
                                                                          TMP[:, hs], ALU.mult, ALU.add),
               reads=[(pid, h), "LNBC", ("TMP", h)], writes=[("BIAS2", h)])

        def gcols(g):
            return slice(g * GT, (g + 1) * GT)

        def prev_buf(par, kc):
            if par == 0:
                return YT[:, kc, :], lambda j: ("YT", kc, j)
            if kc < 4:
                return GU[:, kc, :], lambda j: ("GU", kc)
            return QB[:, kc - 4, :], lambda j: ("QB", kc - 4)

        def hgrn_fm(g, main, prev=False, pbase=0, heads=(0, 1, 2, 3)):
            def src_ap(kc):
                return prev_buf(g % 2, kc)[0] if prev else HT[:, kc, gcols(g)]

            def src_ids(kc):
                return [prev_buf(g % 2, kc)[1](j) for j in range(TPG)] if prev else htg(g, kc)
            for h in heads:
                THx, thn = (TH, "TH") if h % 2 == 0 else (TH2, "TH2")
                KKx, kkn = (KK, "KK") if h % 2 == 0 else (KK2, "KK2")
                pz, pzid = big()
                for kc in range(8):
                    op("pe", lambda e, kc=kc, h=h, pz=pz, THx=THx, KKx=KKx: e.matmul(pz[:, 0:GT], WIN[:, kc, 1536 + h * 128:1536 + (h + 1) * 128],
                                                                   src_ap(kc), start=(kc == 0), stop=(kc == 7)),
                       reads=[("WIN", 3, kc)] + src_ids(kc), writes=Q(pzid))
                op("act", lambda e, pz=pz, THx=THx, KKx=KKx: e.activation(THx[:], pz[:, 0:GT], AF.Tanh, scale=0.5), reads=Q(pzid), writes=[thn])
                op("act", lambda e, h=h, THx=THx, KKx=KKx: e.activation(KKx[:], THx[:], AF.Identity, bias=C1C[:, h:h + 1], scale=NC1C[:, h:h + 1]),
                   reads=[thn, "NC1C", "C1C"], writes=[kkn])
                op("act", lambda e, h=h, THx=THx, KKx=KKx: e.activation(THx[:], THx[:], AF.Identity, bias=C0C[:, h:h + 1], scale=C1C[:, h:h + 1]),
                   reads=[thn, "C1C", "C0C"], writes=[thn])
                for b in range(2 * TPG):
                    sl = slice(b * 64, (b + 1) * 64)
                    op("dve", lambda e, sl=sl, THx=THx, KKx=KKx: e.tensor_tensor_scan(PM[:, sl], THx[:, sl], ZER[:], 1.0, ALU.mult, ALU.add),
                       reads=[thn, "ZER"], writes=[("PM", b)])
                for j in range(TPG):
                    idx = ((g * TPG + j) % NT) * 4 + h
                    c = slice(idx, idx + 1)
                    cid = ("CDEC", idx)
                    pmj = [("PM", 2 * j), ("PM", 2 * j + 1)]
                    op("dve", lambda e, j=j, c=c, THx=THx, KKx=KKx: e.tensor_copy(C63[:, c], PM[:, j * 128 + 63:j * 128 + 64]),
                       reads=pmj, writes=[cid])
                    op("dve", lambda e, j=j, c=c, THx=THx, KKx=KKx: e.tensor_copy(C127[:, c], PM[:, j * 128 + 127:j * 128 + 128]),
                       reads=pmj, writes=[cid])
                    op("dve", lambda e, c=c, THx=THx, KKx=KKx: e.reciprocal(R1[:, c], C63[:, c]), reads=[cid], writes=[cid])
                    op("dve", lambda e, c=c, THx=THx, KKx=KKx: e.tensor_tensor(PLAST[:, c], C63[:, c], C127[:, c], ALU.mult),
                       reads=[cid], writes=[cid])
                    op("dve", lambda e, j=j, c=c, THx=THx, KKx=KKx: e.tensor_scalar(PM[:, j * 128:j * 128 + 64], PM[:, j * 128:j * 128 + 64],
                                                                  R1[:, c], None, ALU.mult),
                       reads=[cid, ("PM", 2 * j)], writes=[("PM", 2 * j)])
                pmall = [("PM", b) for b in range(2 * TPG)]
                op("dve", lambda e, THx=THx, KKx=KKx: e.reciprocal(THx[:], PM[:]), reads=pmall, writes=[thn])
                op("dve", lambda e, h=h, THx=THx, KKx=KKx: e.tensor_tensor(KE[:, h, :], KKx[:], THx[:], ALU.mult),
                   reads=[kkn, thn], writes=[("KE", h)])
                if main:
                    pq, pqid = big()
                    for kc in range(8):
                        op("pe", lambda e, kc=kc, h=h, pq=pq: e.matmul(pq[:, 0:GT], WIN[:, kc, 1024 + h * 128:1024 + (h + 1) * 128],
                                                                       HT[:, kc, gcols(g)], start=(kc == 0), stop=(kc == 7)),
                           reads=[("WIN", 2, kc)] + htg(g, kc), writes=Q(pqid))
                    KQ = int(os.environ.get("KQ", "9"))
                    if KQ >= 2:
                        op("act", lambda e, pq=pq: e.activation(THQ[:], pq[:, 0:GT], AF.Tanh, scale=0.5), reads=Q(pqid), writes=["THQ"])
                    op("dve", lambda e, pq=pq: e.scalar_tensor_tensor(THQ[:], THQ[:], 1.0, pq[:, 0:GT], ALU.add, ALU.mult),
                       reads=Q(pqid) + ["THQ"], writes=["THQ"])
                    op("dve", lambda e, h=h: e.scalar_tensor_tensor(QB[:, h, :], THQ[:], 0.5, PM[:], ALU.mult, ALU.mult),
                       reads=["THQ"] + pmall, writes=[("QB", h)])

        def hgrn_tile(g, j, main, prev=False):
            t = g * TPG + j
            lc = slice(j * 128, (j + 1) * 128)
            def tsrc_ap(kc):
                return prev_buf(g % 2, kc)[0][:, lc] if prev else HT[:, kc, t * 128:(t + 1) * 128]

            def tid(kc):
                return prev_buf(g % 2, kc)[1](j) if prev else ("HT", t, kc)
            pv, pvid = big()
            for kc in range(8):
                op("pe", lambda e, kc=kc, pv=pv: e.matmul(pv[:], tsrc_ap(kc), WIN[:, kc, 2048:2560],
                                                          start=(kc == 0), stop=(kc == 7)),
                   reads=[("WIN", 4, kc), tid(kc)], writes=Q(pvid))
            if prev:
                pc = slice(g * TPG + j, g * TPG + j + 1)
                op("act", lambda e, pv=pv, pc=pc: e.activation(VT[:], pv[:], AF.Identity, scale=PMT[:, pc]),
                   reads=Q(pvid) + ["PMT"], writes=["VT"])
            else:
                op("act", lambda e, pv=pv: e.copy(VT[:], pv[:]), reads=Q(pvid), writes=["VT"])
            pt, ptid = ptb()
            for h in range(4):
                op("pe", lambda e, h=h, pt=pt: e.transpose(pt[:, h, :], KE[:, h, lc], IDB[:]),
                   reads=[("KE", h), "IDB"], writes=[(ptid, h)])
            op("act", lambda e, pt=pt: e.copy(KET[:].rearrange("p (h c) -> p h c", h=4), pt[:, 0:4, :]),
               reads=[(ptid, h) for h in range(4)], writes=["KET"])
            for h in range(4):
                hs = slice(h * 128, (h + 1) * 128)
                idx = (t % NT) * 4 + h
                c = slice(idx, idx + 1)
                cid = ("CDEC", idx)
                op("pe", lambda e, hs=hs: e.matmul(PU[:, hs], KET[:, hs], VT[:, hs], start=True, stop=True),
                   reads=["KET", "VT"], writes=[("PU", h)])
                if main:
                    op("pe", lambda e, h=h, hs=hs: e.matmul(PA[:, hs], KE[:, h, lc], QB[:, h, lc], start=True, stop=True),
                       reads=[("KE", h), ("QB", h)], writes=[("PA", h)])
                    op("dve", lambda e, h=h, hs=hs: e.copy_predicated(ATM[:, h, :], CMASK[:], PA[:, hs]),
                       reads=[("PA", h), "CMASK"], writes=[("ATM", h)])
                    op("dve", lambda e, hs=hs, c=c: e.tensor_scalar(SBF[:, hs], SST[:, hs], C63[:, c], None, ALU.mult),
                       reads=[("S", h), cid], writes=[("SBF", h)])
                    op("pe", lambda e, h=h, hs=hs: e.matmul(PO[:, hs], ATM[:, h, :], VT[:, hs], start=True, stop=False),
                       reads=[("ATM", h), "VT"], writes=[("PO", h)])
                    op("pe", lambda e, h=h, hs=hs: e.matmul(PO[:, hs], QB[:, h, lc], SBF[:, hs], start=False, stop=True),
                       reads=[("QB", h), ("SBF", h)], writes=[("PO", h)])
                op("dve", lambda e, hs=hs, c=c: e.tensor_scalar(TMP[:, hs], PU[:, hs], C127[:, c], None, ALU.mult),
                   reads=[("PU", h), cid], writes=[("TMP", h)])
                op("dve", lambda e, hs=hs, c=c: e.scalar_tensor_tensor(SST[:, hs], SST[:, hs], PLAST[:, c], TMP[:, hs],
                                                                       ALU.mult, ALU.add),
                   reads=[("S", h), cid, ("TMP", h)], writes=[("S", h)])

        def rev(ap, n):
            return bass.AP(ap.tensor, ap.offset + n - 1, [list(ap.ap[0]), [-1, n]])

        def zeros_ap(n):
            z = ZER[:, 0:1]
            return bass.AP(z.tensor, z.offset, [list(z.ap[0]), [0, n]])

        def hgrn_fm_prev(pg_, heads):
            for h in heads:
                THx, thn = (TH, "TH") if h % 2 == 0 else (TH2, "TH2")
                KKx, kkn = (KK, "KK") if h % 2 == 0 else (KK2, "KK2")
                pz, pzid = big()
                for kc in range(8):
                    sap, sidf = prev_buf(pg_ % 2, kc)
                    op("pe", lambda e, kc=kc, h=h, pz=pz, sap=sap: e.matmul(pz[:, 0:GT], WIN[:, kc, 1536 + h * 128:1536 + (h + 1) * 128],
                                                                            sap, start=(kc == 0), stop=(kc == 7)),
                       reads=[("WIN", 3, kc)] + [sidf(j) for j in range(TPG)], writes=Q(pzid))
                op("act", lambda e, pz=pz, THx=THx: e.activation(THx[:], pz[:, 0:GT], AF.Tanh, scale=0.5), reads=Q(pzid), writes=[thn])
                op("act", lambda e, h=h, THx=THx, KKx=KKx: e.activation(KKx[:], THx[:], AF.Identity, bias=C1C[:, h:h + 1], scale=NC1C[:, h:h + 1]),
                   reads=[thn, "NC1C", "C1C"], writes=[kkn])
                op("act", lambda e, h=h, THx=THx: e.activation(THx[:], THx[:], AF.Identity, bias=C0C[:, h:h + 1], scale=C1C[:, h:h + 1]),
                   reads=[thn, "C1C", "C0C"], writes=[thn])
                for j in range(TPG):
                    t0 = j * 128
                    idx = ((pg_ * TPG + j) % NT) * 4 + h
                    c = slice(idx, idx + 1)
                    cid = ("CDEC", idx)
                    pmj = [("PM", 2 * j), ("PM", 2 * j + 1)]
                    op("dve", lambda e, t0=t0, THx=THx: e.tensor_tensor_scan(rev(PM[:, t0:t0 + 128], 128), rev(THx[:, t0:t0 + 128], 128),
                                                                             zeros_ap(128), 1.0, ALU.mult, ALU.add),
                       reads=[thn, "ZER"], writes=pmj)
                    op("dve", lambda e, t0=t0, h=h, KKx=KKx: e.tensor_tensor(KE[:, h, t0:t0 + 127], KKx[:, t0:t0 + 127], PM[:, t0 + 1:t0 + 128], ALU.mult),
                       reads=[kkn] + pmj, writes=[("KE", h)])
                    op("dve", lambda e, t0=t0, h=h, KKx=KKx: e.tensor_copy(KE[:, h, t0 + 127:t0 + 128], KKx[:, t0 + 127:t0 + 128]),
                       reads=[kkn], writes=[("KE", h)])
                    op("dve", lambda e, t0=t0, c=c: e.tensor_copy(PLAST[:, c], PM[:, t0:t0 + 1]), reads=pmj, writes=[cid])

        def hgrn_tile_prev(pg_, j):
            lc = slice(j * 128, (j + 1) * 128)
            t = pg_ * TPG + j
            pv, pvid = big()
            for kc in range(8):
                sap, sidf = prev_buf(pg_ % 2, kc)
                op("pe", lambda e, kc=kc, pv=pv, sap=sap: e.matmul(pv[:], sap[:, lc], WIN[:, kc, 2048:2560], start=(kc == 0), stop=(kc == 7)),
                   reads=[("WIN", 4, kc), sidf(j)], writes=Q(pvid))
            pc = slice(t, t + 1)
            op("act", lambda e, pv=pv, pc=pc: e.activation(VT[:], pv[:], AF.Identity, scale=PMT[:, pc]),
               reads=Q(pvid) + ["PMT"], writes=["VT"])
            pt, ptid = ptb()
            for h in range(4):
                op("pe", lambda e, h=h, pt=pt: e.transpose(pt[:, h, :], KE[:, h, lc], IDB[:]),
                   reads=[("KE", h), "IDB"], writes=[(ptid, h)])
            op("act", lambda e, pt=pt: e.copy(KET[:].rearrange("p (h c) -> p h c", h=4), pt[:, 0:4, :]),
               reads=[(ptid, h) for h in range(4)], writes=["KET"])
            for h in range(4):
                hs = slice(h * 128, (h + 1) * 128)
                op("pe", lambda e, hs=hs: e.matmul(PU[:, hs], KET[:, hs], VT[:, hs], start=True, stop=True),
                   reads=["KET", "VT"], writes=[("PU", h)])
            for h in range(4):
                hs = slice(h * 128, (h + 1) * 128)
                idx = (t % NT) * 4 + h
                c = slice(idx, idx + 1)
                op("dve", lambda e, hs=hs, c=c: e.scalar_tensor_tensor(SST[:, hs], SST[:, hs], PLAST[:, c], PU[:, hs], ALU.mult, ALU.add),
                   reads=[("S", h), ("CDEC", idx), ("PU", h)], writes=[("S", h)])

        def build_ht_prev(p, j, par):
            if p % 2 == 0:
                op("sp", lambda e: e.dma_start(out=GBC[:], in_=xprev_d[p * 128:(p + 1) * 128, :]),
                   writes=[("GBC", 0), ("GBC", 1)], dma="xprev")
                op("act", lambda e: e.activation(XN[:], GBC[:], AF.Identity, scale=RSP[:, p:p + 1]),
                   reads=[("GBC", 0), ("GBC", 1), ("RSP", p // 16)], writes=["XN"])
                XS, xsid = XN, "XN"
            else:
                op("sp", lambda e: e.dma_start(out=GV[:], in_=xprev_d[p * 128:(p + 1) * 128, 0:512]),
                   writes=Q("GV"), dma="xprevb")
                op("sp", lambda e: e.dma_start(out=TMP[:], in_=xprev_d[p * 128:(p + 1) * 128, 512:1024]),
                   writes=Q("TMP"), dma="xprevc")
                op("act", lambda e: e.activation(SQJ[:, 0:512], GV[:], AF.Identity, scale=RSP[:, p:p + 1]),
                   reads=Q("GV") + [("RSP", p // 16)], writes=["SQJ"])
                op("act", lambda e: e.activation(SQJ[:, 512:1024], TMP[:], AF.Identity, scale=RSP[:, p:p + 1]),
                   reads=Q("TMP") + [("RSP", p // 16)], writes=["SQJ"])
                XS, xsid = SQJ, "SQJ"
            pt, ptid = ptb()
            for kc in range(8):
                op("pe", lambda e, kc=kc: e.transpose(pt[:, kc, :], XS[:, kc * 128:(kc + 1) * 128], IDB[:]),
                   reads=[xsid, "IDB"], writes=[(ptid, kc)])
            for kc in range(8):
                bap, bid = prev_buf(par, kc)
                dst = bap[:, j * 128:(j + 1) * 128]
                wid = bid(j)
                op("act", lambda e, kc=kc, dst=dst: e.activation(dst, pt[:, kc, :], AF.Identity,
                                                                 bias=SH1C[:, kc:kc + 1], scale=A1C[:, kc:kc + 1]),
                   reads=[(ptid, kc), "A1C", "SH1C"], writes=[wid])

        if STAGE >= 2:
            npg = NPREV // TPG
            for j in range(TPG):
                build_ht_prev(j, j, 0)
            for pg_ in range(npg):
                hgrn_fm_prev(pg_, (0, 1))
                if pg_ + 1 < npg:
                    for j in range(TPG):
                        build_ht_prev((pg_ + 1) * TPG + j, j, (pg_ + 1) % 2)
                hgrn_fm_prev(pg_, (2, 3))
                for j in range(TPG):
                    hgrn_tile_prev(pg_, j)

        en[0] = STAGE >= 1.5
        to_wout(wada_rows(2), "wo")
        ada_bc(WOUT, lambda kc: [("WOUT", kc, 0), ("WOUT", kc, 1)], 0)
        to_wout(lambda kc: wout_d[kc * 128:(kc + 1) * 128, :], "wo")
        for kc in range(8):
            for hh in range(2):
                cs_ = slice(hh * 512, (hh + 1) * 512)
                op("pool", lambda e, kc=kc, cs_=cs_: e.tensor_tensor(WOUT[:, kc, cs_], WOUT[:, kc, cs_], GBC[:, cs_], ALU.mult),
                   reads=[("WOUT", kc, hh), ("GBC", hh)], writes=[("WOUT", kc, hh)])
        en[0] = True

        KG = int(os.environ.get('KG', str(NG)))
        for g in (range(KG) if STAGE >= 4 else []):
            for fc in range(4):
                pu, puid = big()
                for kc in range(8):
                    op("pe", lambda e, kc=kc, fc=fc, pu=pu, g=g: e.matmul(pu[:, 0:GT], WIN[:, kc, fc * 128:(fc + 1) * 128],
                                                                          HT[:, kc, gcols(g)], start=(kc == 0), stop=(kc == 7)),
                       reads=[("WIN", 0, kc)] + htg(g, kc), writes=Q(puid))
                op("act", lambda e, fc=fc, pu=pu: e.activation(GU[:, fc, :], pu[:, 0:GT], AF.Gelu), reads=Q(puid), writes=[("GU", fc)])
            if STAGE >= 4.1:
                hgrn_fm(g, not os.environ.get('KFM0'))
            for j in (range(TPG) if STAGE >= 4.2 else []):
                t = g * TPG + j
                lc = slice(j * 128, (j + 1) * 128)
                pv, pvid = big()
                for kc in range(8):
                    op("pe", lambda e, kc=kc, t=t, pv=pv: e.matmul(pv[:], HT[:, kc, t * 128:(t + 1) * 128], WIN[:, kc, 512:1024],
                                                                   start=(kc == 0), stop=(kc == 7)),
                       reads=[("WIN", 1, kc), ("HT", t, kc)], writes=Q(pvid))
                op("act", lambda e, pv=pv: e.activation(GV[:], pv[:], AF.Gelu), reads=Q(pvid), writes=Q("GV"))
                op("dve", lambda e: e.bn_stats(BNS[:], GV[:]), reads=Q("GV"), writes=["BNS"])
                op("dve", lambda e: e.bn_aggr(MV[:], BNS[:]), reads=["BNS"], writes=["MV"])
                op("dve", lambda e: e.tensor_scalar(NWV[:, 0:1], MV[:, 1:2], 1.0, EPS, ALU.mult, ALU.add),
                   reads=["MV"], writes=["NWV"])
                newton_rsqrt(NWV[:, 0:1], RSV[:], NWT[:, 0:1], "NWV", "RSV", "NWT")
                op("dve", lambda e: e.tensor_scalar(VH[:], GV[:], MV[:, 0:1], RSV[:], ALU.subtract, ALU.mult),
                   reads=Q("GV") + ["MV", "RSV"], writes=["VH"])
                pm, pmid = big()
                for h in range(4):
                    hs = slice(h * 128, (h + 1) * 128)
                    op("pe", lambda e, h=h, hs=hs, pm=pm: e.matmul(pm[:, hs], VH[:, hs], WST[:, h, :], start=True, stop=True),
                       reads=["VH", ("WST", h)], writes=[(pmid, h)])
                    op("dve", lambda e, h=h, hs=hs, pm=pm: e.scalar_tensor_tensor(TMP[:, hs], pm[:, hs], LNWC[:, h:h + 1],
                                                                                  BIAS2[:, h, :], ALU.mult, ALU.add),
                       reads=[(pmid, h), "LNWC", ("BIAS2", h)], writes=[("TMP", h)])
                    op(POOLC, lambda e, h=h, hs=hs, lc=lc: e.tensor_tensor(YT[:, h, lc], TMP[:, hs], GU[:, h, lc], ALU.mult),
                       reads=[("TMP", h), ("GU", h)], writes=[("YT", h, j)])
                en[0] = STAGE >= 4.3
                hgrn_tile(g, j, True)
                en[0] = STAGE >= 4.4
                pg, pgid = big()
                for kc in range(8):
                    op("pe", lambda e, kc=kc, t=t, pg=pg: e.matmul(pg[:], HT[:, kc, t * 128:(t + 1) * 128], WIN[:, kc, 2560:3072],
                                                                   start=(kc == 0), stop=(kc == 7)),
                       reads=[("WIN", 5, kc), ("HT", t, kc)], writes=Q(pgid))
                op("act", lambda e, pg=pg: e.activation(GV[:], pg[:], AF.Tanh, scale=0.5), reads=Q(pgid), writes=Q("GV"))
                op("dve", lambda e, pg=pg: e.scalar_tensor_tensor(GV[:], GV[:], 1.0, pg[:], ALU.add, ALU.mult),
                   reads=Q(pgid) + Q("GV"), writes=Q("GV"))
                op(POOLC, lambda e: e.tensor_tensor(GV[:], GV[:], GNWBC[:], ALU.mult), reads=Q("GV") + ["GNWBC"], writes=Q("GV"))
                for h in range(4):
                    hs = slice(h * 128, (h + 1) * 128)
                    op("act", lambda e, h=h, hs=hs: e.activation(SQJ[:, hs], PO[:, hs], AF.Square, accum_out=SSO[:, h:h + 1]),
                       reads=[("PO", h)], writes=["SQJ", ("SSO", h)])
                op("dve", lambda e: e.tensor_scalar(NWV[:, 0:4], SSO[:], 1.0 / 128, EPS, ALU.mult, ALU.add),
                   reads=[("SSO", h) for h in range(4)], writes=["NWV"])
                newton_rsqrt(NWV[:, 0:4], RSO[:], NWT[:, 0:4], "NWV", "RSO", "NWT")
                for h in range(4):
                    hs = slice(h * 128, (h + 1) * 128)
                    op("dve", lambda e, h=h, hs=hs: e.scalar_tensor_tensor(YB[:, hs], PO[:, hs], RSO[:, h:h + 1], GV[:, hs],
                                                                           ALU.mult, ALU.mult),
                       reads=[("PO", h), "RSO", ("GV", h)], writes=[("YB", h)])
                pt, ptid = ptb()
                for h in range(4):
                    hs = slice(h * 128, (h + 1) * 128)
                    op("pe", lambda e, h=h, hs=hs, pt=pt: e.transpose(pt[:, 4 + h, :], YB[:, hs], IDB[:]),
                       reads=[("YB", h), "IDB"], writes=[(ptid, 4 + h)])
                op("act", lambda e, lc=lc, pt=pt: e.copy(YT[:, 4:8, lc], pt[:, 4:8, :]),
                   reads=[(ptid, 4 + h) for h in range(4)], writes=[("YT", 4 + h, j) for h in range(4)])
                en[0] = STAGE >= 4.5
                for half in range(2):
                    pw, pwid = big()
                    cs = slice(half * 512, (half + 1) * 512)
                    for kc in range(8):
                        op("pe", lambda e, kc=kc, cs=cs, lc=lc, pw=pw: e.matmul(pw[:], YT[:, kc, lc], WOUT[:, kc, cs],
                                                                                 start=(kc == 0), stop=(kc == 7)),
                           reads=[("YT", kc, j), ("WOUT", kc, half)], writes=Q(pwid))
                    op("dve", lambda e, cs=cs, pw=pw, t=t: e.tensor_tensor(X[:, t, cs], X[:, t, cs], pw[:], ALU.add),
                       reads=Q(pwid) + [("X", t)], writes=[("X", t)])
                en[0] = STAGE >= 4.6
                op("act", lambda e, t=t: e.activation(SQJ[:], X[:, t, :], AF.Square, accum_out=SS2[:]),
                   reads=[("X", t)], writes=["SQJ", "SS2"])
                op("dve", lambda e: e.tensor_scalar(NWV[:, 0:1], SS2[:], 1.0 / D, EPS, ALU.mult, ALU.add),
                   reads=["SS2"], writes=["NWV"])
                newton_rsqrt(NWV[:, 0:1], RS2[:], NWT[:, 0:1], "NWV", "RS2", "NWT")
                build_ht(t, RS2[:], "RS2", A2C, SH2C, "A2C", "SH2C")
                en[0] = True

        def do_dumps(names, base):
            en[0] = True
            dmap = {
                "GU0": (GU[:, 0, :], [("GU", 0)]), "KE0": (KE[:, 0, :], [("KE", 0)]), "QB0": (QB[:, 0, :], [("QB", 0)]),
                "PM": (PM[:], [("PM", b) for b in range(2 * TPG)]), "VT": (VT[:], ["VT"]), "GV": (GV[:], Q("GV")),
                "YB": (YB[:], [("YB", h) for h in range(4)]), "YTA": (YT[:, 0, :], [("YT", 0, j) for j in range(TPG)]),
                "YTB": (YT[:, 4, :], [("YT", 4, j) for j in range(TPG)]), "SST": (SST[:], [("S", h) for h in range(4)]),
                "X1": (X[:, 1, 0:512], [("X", 1)]), "HT1": (HT[:, 0, 128:256], [("HT", 1, 0)]),
                "HT0": (HT[:, 0, 0:128], [("HT", 0, 0)]), "VH": (VH[:], ["VH"]), "KET": (KET[:], ["KET"]),
                "ATM0": (ATM[:, 0, :], [("ATM", 0)]), "SBF": (SBF[:], [("SBF", h) for h in range(4)]),
                "C63": (C63[:], [("CDEC", i) for i in range(NT * 4)]), "C127": (C127[:], [("CDEC", i) for i in range(NT * 4)]),
                "RSO": (RSO[:], ["RSO"]), "GBC": (GBC[:, 0:512], [("GBC", 0)]), "COLS": (COLS[:], ["A1C", "SH1C", "A2C", "SH2C", "C0C", "C1C"]),
                "BIAS2": (BIAS2[:, 0, :], [("BIAS2", 0)]), "TH": (TH[:], ["TH"]), "KK": (KK[:], ["KK"]), "THQ": (THQ[:], ["THQ"]),
            }
            for i, nm in enumerate(names):
                ap, ids = dmap[nm]
                w = ap.shape[-1]
                op("dve", lambda e, ap=ap, w=w: e.tensor_copy(TMP[:, 0:w], ap), reads=ids, writes=Q("TMP"))
                op("sp", lambda e, i=i, w=w: e.dma_start(out=dbg_d[(base + i) * 128:(base + i + 1) * 128, 0:w], in_=TMP[:, 0:w]),
                   reads=Q("TMP"), writes=["dbgout"], dma="dbg")

        if DUMPS:
            do_dumps([d[4:] for d in DUMPS if d.startswith('pre_')], 0)
        if STAGE >= 5:
            to_wout(wada_rows(5), "wo")
            ada_bc(WOUT, lambda kc: [("WOUT", kc, 0), ("WOUT", kc, 1)], 1)

        bigs.extend([(PA, "PA"), (PO, "PO"), (PU, "PU")])
        ACTT = [GU, KE]
        ACTN = ["GU", "KE"]
        allwin = [("WIN", b, kc) for b in range(6) for kc in range(8)]
        WOF = WOUT[:].rearrange("p k c -> p (k c)")
        ACTB = [WOF[:, i * 2048:(i + 1) * 2048].rearrange("p (c t) -> p c t", c=4) for i in range(4)]
        if STAGE >= 5:
            op("pool", lambda e: e.memset(NWT2[:, 1:2], 0.0),
               writes=[("WOUT", kc, hh) for kc in range(8) for hh in range(2)] + ["WOUTFREE", ("NWT2", 0)])
        KP = int(os.environ.get('KP', '6'))
        for pi, (c0, n) in enumerate(FFN_PASSES[:KP] if STAGE >= 5 else []):
            slot = pi % 2
            base = slot * 12288
            WG = WINF[:, base:base + 4096].rearrange("p (k c) -> p k c", k=8)
            WU = WINF[:, base + 4096:base + 8192].rearrange("p (k c) -> p k c", k=8)
            WO = WINF[:, base + 8192:base + 12288].rearrange("p (c j) -> p c j", c=4)
            if pi == 0:
                op("pool", lambda e: e.memset(NWT2[:, 0:1], 0.0), writes=allwin + ["WINFREE", ("NWT2", 0)])
            extra = []
            gctr[0] += 1
            for kc in range(8):
                op("pool", lambda e, WG=WG, c0=c0, n=n, kc=kc: e.dma_start(
                    out=WG[:, kc, 0:n * 128], in_=wfi_d[kc * 128:(kc + 1) * 128, c0 * 128:(c0 + n) * 128]),
                   reads=["WINFREE"], writes=[("FWG", slot, kc)], dma="fwg%d" % slot, grp=gctr[0])
            for kc in range(8):
                op("pool", lambda e, WU=WU, c0=c0, n=n, kc=kc: e.dma_start(
                    out=WU[:, kc, 0:n * 128], in_=wfi_d[kc * 128:(kc + 1) * 128, DFF + c0 * 128:DFF + (c0 + n) * 128]),
                   reads=["WINFREE"], writes=[("FWU", slot, kc)], dma="fwu%d" % slot, grp=gctr[0])
            for ci in range(n):
                for hh in range(2):
                    op("pool", lambda e, WO=WO, c0=c0, ci=ci, hh=hh: e.dma_start(
                        out=WO[:, ci, hh * 512:(hh + 1) * 512],
                        in_=wfo_d[(c0 + ci) * 128:(c0 + ci + 1) * 128, hh * 512:(hh + 1) * 512]),
                       reads=["WINFREE"], writes=[("FWO", slot, ci, hh)], dma="fwo%d" % slot, grp=gctr[0])
            for ci in range(n):
                for hh in range(2):
                    cs_ = slice(hh * 512, (hh + 1) * 512)
                    op("pool", lambda e, WO=WO, ci=ci, cs_=cs_: e.tensor_tensor(WO[:, ci, cs_], WO[:, ci, cs_], GBC[:, cs_], ALU.mult),
                       reads=[("FWO", slot, ci, hh), ("GBC", hh)], writes=[("FWO", slot, ci, hh)])
            for gf in range(4):
                bi = (pi * 4 + gf) % 4
                at = ACTB[bi]
                hids = lambda kc, gf=gf: [("HT", gf * 4 + j, kc) for j in range(4)]
                for ci in range(n):
                    pg, pgid = big()
                    pu, puid = big()
                    for kc in range(8):
                        op("pe", lambda e, kc=kc, ci=ci, WG=WG, pg=pg, gf=gf: e.matmul(
                            pg[:], WG[:, kc, ci * 128:(ci + 1) * 128], HT[:, kc, gf * 512:(gf + 1) * 512], start=(kc == 0), stop=(kc == 7)),
                           reads=[("FWG", slot, kc)] + hids(kc), writes=Q(pgid))
                    for kc in range(8):
                        op("pe", lambda e, kc=kc, ci=ci, WU=WU, pu=pu, gf=gf: e.matmul(
                            pu[:], WU[:, kc, ci * 128:(ci + 1) * 128], HT[:, kc, gf * 512:(gf + 1) * 512], start=(kc == 0), stop=(kc == 7)),
                           reads=[("FWU", slot, kc)] + hids(kc), writes=Q(puid))
                    SIL, silid = (GV, Q("GV")) if ci % 2 == 0 else (TMP, Q("TMP"))
                    op("act", lambda e, pg=pg, SIL=SIL: e.activation(SIL[:], pg[:], AF.Silu), reads=Q(pgid), writes=silid)
                    op("dve", lambda e, ci=ci, at=at, pu=pu, SIL=SIL: e.tensor_tensor(at[:, ci, :], SIL[:], pu[:], ALU.mult),
                       reads=silid + Q(puid) + ["WOUTFREE"], writes=[("ACTT", bi, ci)])
                for j in range(4):
                    t = gf * 4 + j
                    lc = slice(j * 128, (j + 1) * 128)
                    for half in range(2):
                        cs = slice(half * 512, (half + 1) * 512)
                        po, poid = big()
                        for ci in range(n):
                            op("pe", lambda e, ci=ci, at=at, WO=WO, po=po, lc=lc, cs=cs, n=n: e.matmul(
                                po[:], at[:, ci, lc], WO[:, ci, cs], start=(ci == 0), stop=(ci == n - 1)),
                               reads=[("ACTT", bi, ci), ("FWO", slot, ci, half)], writes=Q(poid))
                        op("dve", lambda e, po=po, cs=cs, t=t: e.tensor_tensor(X[:, t, cs], X[:, t, cs], po[:], ALU.add),
                           reads=Q(poid) + [("X", t)], writes=[("X", t)])

        if DUMPS:
            do_dumps([d for d in DUMPS if not d.startswith('pre_')], len([d for d in DUMPS if d.startswith('pre_')]))
        en[0] = True
        load("sp", GV[:], fnw_d[:, 0:512], Q("GV"))
        load("sp", TMP[:], fnw_d[:, 512:1024], Q("TMP"))
        FNW = [GV, TMP]
        FNWID = [Q("GV"), Q("TMP")]
        for t in range(NT):
            op("act", lambda e, t=t: e.activation(SQJ[:], X[:, t, :], AF.Square, accum_out=SS1[:, t:t + 1]),
               reads=[("X", t)], writes=["SQJ", ("SS1", t)])
        op("dve", lambda e: e.tensor_scalar(NWV[:], SS1[:], 1.0 / D, EPS, ALU.mult, ALU.add),
           reads=[("SS1", t) for t in range(NT)], writes=["NWV"])
        newton_rsqrt(NWV[:], RS1[:], NWT[:], "NWV", "RS1", "NWT")
        for t in range(NT):
            for half in range(2):
                cs = slice(half * 512, (half + 1) * 512)
                op("dve", lambda e, t=t, cs=cs, half=half: e.scalar_tensor_tensor(X[:, t, cs], X[:, t, cs], RS1[:, t:t + 1],
                                                                                  FNW[half][:], ALU.mult, ALU.mult),
                   reads=[("X", t), "RS1"] + FNWID[half], writes=[("X", t)])
            op("sp", lambda e, t=t: e.dma_start(out=out_d[t * 128:(t + 1) * 128, :], in_=X[:, t, :]),
               reads=[("X", t)], dma="st%d" % (t % 4))

        S.emit(st)
    return nc


_NC_CACHE = {}


def _col(v, n):
    return np.ascontiguousarray(np.asarray(v, np.float32).reshape(n, 128).T)


def kernel(x, c, w_ada, b_ada, norm1_w, w_in, w_s, b_s, v_ln_w, v_ln_b,
           lower_bounds, gn_w, w_out, norm2_w, w_ffn_in, w_ffn_out, final_norm_w):
    f32 = lambda a: np.ascontiguousarray(np.asarray(a, dtype=np.float32))
    x = f32(x); c = f32(c)
    w_ada0 = f32(w_ada)[0]; b_ada0 = f32(b_ada)[0]
    w_in0 = f32(w_in)[0]; w_out0 = f32(w_out)[0]
    wfi0 = f32(w_ffn_in)[0]; wfo0 = f32(w_ffn_out)[0]
    w_s0 = f32(w_s)[0]; b_s0 = f32(b_s)[0]
    lb = f32(lower_bounds)
    if "nc" not in _NC_CACHE:
        _NC_CACHE["nc"] = build_nc()
    nc = _NC_CACHE["nc"]
    s_idx = np.arange(128)
    cid = s_idx // 64
    common = {
        "w_ada": w_ada0,
        "b_ada_col": _col(b_ada0, 48),
        "b_ada_bc": np.ascontiguousarray(np.broadcast_to(
            np.concatenate([b_ada0[2 * D:3 * D], b_ada0[5 * D:6 * D]])[None, :], (128, 2 * D))),
        "n1_col": _col(f32(norm1_w)[0], 8),
        "n2_col": _col(f32(norm2_w)[0], 8),
        "w_in": w_in0,
        "wsT": np.ascontiguousarray(w_s0.transpose(2, 0, 1).reshape(128, 512)),
        "bs_bc": np.ascontiguousarray(np.broadcast_to(b_s0.reshape(1, 512), (128, 512))),
        "lnw_col": _col(f32(v_ln_w)[0], 4),
        "lnb_col": _col(f32(v_ln_b)[0], 4),
        "lba0": _col(lb[0], 4),
        "lba1": _col(lb[1], 4),
        "gnw_bc": np.ascontiguousarray(np.broadcast_to(np.tile(f32(gn_w)[0], 4)[None, :], (128, 512))),
        "w_out": w_out0,
        "w_ffn_in": wfi0,
        "w_ffn_out": wfo0,
        "fnw_bc": np.ascontiguousarray(np.broadcast_to(f32(final_norm_w)[None, :], (128, D))),
        "ident": np.eye(128, dtype=np.float32).astype(ml_dtypes.bfloat16),
        "cmask": (s_idx[:, None] <= s_idx[None, :]).astype(np.int32),
        "gmask": (cid[None, :] >= cid[:, None]).astype(np.float32),
    }
    in_maps = []
    for r in range(NCORES):
        b, seg = r // 4, r % 4
        pm = np.zeros((128, 8), np.float32)
        for j in range(NCORES):
            if j // 4 == b and j < r:
                pm[:, j] = 1.0
        m = dict(common)
        m["x"] = np.ascontiguousarray(x[b, seg * TOK:(seg + 1) * TOK, :])
        m["c_col"] = _col(c[b], 8)
        m["pmask"] = pm
        xp = np.zeros((NPREV * 128, D), np.float32)
        pmt = np.zeros((128, NPREV), np.float32)
        npred = seg * TOK
        if npred:
            xp[NPREV * 128 - npred:] = x[b, 0:npred, :]
            pmt[:, NPREV - npred // 128:] = 1.0
        m["xprev"] = xp
        m["pmt"] = pmt
        in_maps.append(m)
    res = run_bass_kernel_spmd(nc, in_maps, core_ids=list(range(NCORES)))
    out = np.empty((2, SEQ, D), np.float32)
    if 'dbg' in res.results[0]:
        _NC_CACHE['dbg'] = [np.asarray(r['dbg']) for r in res.results]
    for r in range(NCORES):
        b, seg = r // 4, r % 4
        out[b, seg * TOK:(seg + 1) * TOK, :] = np.asarray(res.results[r]["out"], dtype=np.float32)
    return out
```
